# Optimizing a Trainium2 kernel written in Bass

```python
import jax, jax.numpy as jnp
from jax import lax
import numpy as np

D_MODEL = 1024
BATCH = 4
SEQ = 4096
DEPTH = 1

CHUNK = 128
A_GROUPS = 8
A_WIDTH = 1024
A_GROUP_DIM = A_WIDTH // A_GROUPS
N_HEADS = 8
N_KV_HEADS = 2
HEAD_DIM = 128
IDX_HEADS = 8
IDX_DIM = 64
TOPK_MAX = 256
Q_BLOCK = 128
ROPE_THETA = 500000.0
ROT_DIM = HEAD_DIM // 4
IDX_ROT_DIM = IDX_DIM // 4
D_FF = -(-8 * D_MODEL // (3 * 256)) * 256
EPS = 1e-6
NEG_INF = -1e30

IN_SIZES = (A_WIDTH, A_WIDTH, N_HEADS * HEAD_DIM, N_KV_HEADS * HEAD_DIM,
            N_KV_HEADS * HEAD_DIM, IDX_HEADS * IDX_DIM, IDX_DIM, IDX_HEADS,
            D_MODEL, D_MODEL)
IN_COLS = sum(IN_SIZES)

kernel_name = "hybrid_gated_gmlp_dsa_block"


def rms_norm(x, g):
    xf = x.astype(jnp.float32)
    y = xf * lax.rsqrt(jnp.mean(xf * xf, axis=-1, keepdims=True) + EPS)
    return (y * g.astype(jnp.float32)).astype(x.dtype)


def layer_norm(x, g, b):
    xf = x.astype(jnp.float32)
    mu = jnp.mean(xf, axis=-1, keepdims=True)
    var = jnp.mean(jnp.square(xf - mu), axis=-1, keepdims=True)
    y = (xf - mu) * lax.rsqrt(var + EPS)
    return (y * g.astype(jnp.float32) + b.astype(jnp.float32)).astype(x.dtype)


def partial_rope(x, pos, rot_dim):
    inv_freq = ROPE_THETA ** (-jnp.arange(0, rot_dim, 2, dtype=jnp.float32) / rot_dim)
    ang = pos.astype(jnp.float32)[..., None] * inv_freq
    cos = jnp.cos(ang)[:, :, None, :]
    sin = jnp.sin(ang)[:, :, None, :]
    xr = x[..., :rot_dim].astype(jnp.float32)
    x1, x2 = xr[..., : rot_dim // 2], xr[..., rot_dim // 2:]
    rot = jnp.concatenate([x1 * cos - x2 * sin, x2 * cos + x1 * sin], axis=-1)
    return jnp.concatenate([rot.astype(x.dtype), x[..., rot_dim:]], axis=-1)


def chunked_gmlp(u, v, w_s, b_s, ln_g, ln_b):
    bsz, s, _ = u.shape
    n = s // CHUNK
    v = layer_norm(v, ln_g, ln_b).reshape(bsz, n, CHUNK, A_GROUPS, A_GROUP_DIM)
    mask = jnp.tril(jnp.ones((CHUNK, CHUNK), dtype=w_s.dtype))
    mixed = jnp.einsum('gts,bnsgc->bntgc', w_s * mask, v) + b_s.T[None, None, :, :, None]
    return u * mixed.reshape(bsz, s, A_WIDTH)


def dsa_attention(q, k, v, qi, ki, wi):
    bsz, s = q.shape[0], q.shape[1]
    top_k = min(TOPK_MAX, s // 4)
    n_blocks = s // Q_BLOCK
    key_pos = jnp.arange(s)
    ki32 = ki.astype(jnp.float32)
    gather = jax.vmap(lambda kk, ii: kk[ii])

    def block(i):
        start = i * Q_BLOCK
        qb = lax.dynamic_slice_in_dim(q, start, Q_BLOCK, axis=1)
        qib = lax.dynamic_slice_in_dim(qi, start, Q_BLOCK, axis=1).astype(jnp.float32)
        wib = lax.dynamic_slice_in_dim(wi, start, Q_BLOCK, axis=1).astype(jnp.float32)
        qpos = start + jnp.arange(Q_BLOCK)
        causal = key_pos[None, :] <= qpos[:, None]
        logits = jnp.einsum('bthd,bsd->bths', qib, ki32) * (IDX_DIM ** -0.5)
        score = jnp.einsum('bth,bths->bts', wib * (IDX_HEADS ** -0.5), jax.nn.relu(logits))
        score = jnp.where(causal[None], score, -jnp.inf)
        sel_score, sel_idx = lax.top_k(score, top_k)
        valid = jnp.isfinite(sel_score)
        k_sel = gather(k, sel_idx).astype(jnp.float32)
        v_sel = gather(v, sel_idx).astype(jnp.float32)
        qg = qb.reshape(bsz, Q_BLOCK, N_KV_HEADS, N_HEADS // N_KV_HEADS, HEAD_DIM)
        att = jnp.einsum('btgrd,btkgd->btgrk', qg.astype(jnp.float32), k_sel) * (HEAD_DIM ** -0.5)
        att = jnp.where(valid[:, :, None, None, :], att, NEG_INF)
        p = jax.nn.softmax(att, axis=-1)
        o = jnp.einsum('btgrk,btkgd->btgrd', p, v_sel).astype(q.dtype)
        return o.reshape(bsz, Q_BLOCK, N_HEADS * HEAD_DIM)

    out = lax.map(block, jnp.arange(n_blocks))
    return out.transpose(1, 0, 2, 3).reshape(bsz, s, N_HEADS * HEAD_DIM)


def hybrid_layer(x, c, positions, w_ada, b_ada, norm1_g, w_in, gmlp_ln_g, gmlp_ln_b,
                 gmlp_w_s, gmlp_b_s, idx_k_ln_g, idx_k_ln_b, w_proj_a, w_proj_b, w_out,
                 norm2_g, w_ffn_in, w_ffn_out):
    bsz, s, _ = x.shape
    mod = jax.nn.silu(c) @ w_ada + b_ada
    shift1, scale1, gate1, shift2, scale2, gate2 = [m[:, None, :] for m in jnp.split(mod, 6, axis=-1)]

    h = rms_norm(x, norm1_g) * (1.0 + scale1) + shift1
    z = h @ w_in
    split_points = []
    acc = 0
    for sz in IN_SIZES[:-1]:
        acc += sz
        split_points.append(acc)
    a_u, a_v, q, k, v, qi, ki, wi, g_a, g_b = jnp.split(z, split_points, axis=-1)

    y_a = chunked_gmlp(jax.nn.gelu(a_u), jax.nn.gelu(a_v), gmlp_w_s, gmlp_b_s, gmlp_ln_g, gmlp_ln_b)

    q = partial_rope(q.reshape(bsz, s, N_HEADS, HEAD_DIM), positions, ROT_DIM)
    k = partial_rope(k.reshape(bsz, s, N_KV_HEADS, HEAD_DIM), positions, ROT_DIM)
    v = v.reshape(bsz, s, N_KV_HEADS, HEAD_DIM)
    qi = partial_rope(qi.reshape(bsz, s, IDX_HEADS, IDX_DIM), positions, IDX_ROT_DIM)
    ki = layer_norm(ki, idx_k_ln_g, idx_k_ln_b)
    ki = partial_rope(ki[:, :, None, :], positions, IDX_ROT_DIM)[:, :, 0, :]
    y_b = dsa_attention(q, k, v, qi, ki, wi)

    merged = jax.nn.sigmoid(g_a) * (y_a @ w_proj_a) + jax.nn.sigmoid(g_b) * (y_b @ w_proj_b)
    x = x + gate1 * (merged @ w_out)

    h2 = rms_norm(x, norm2_g) * (1.0 + scale2) + shift2
    f_g, f_u = jnp.split(h2 @ w_ffn_in, 2, axis=-1)
    x = x + gate2 * ((jax.nn.silu(f_g) * f_u) @ w_ffn_out)
    return x


def setup_inputs(seed: int = 0) -> dict:
    key = jax.random.key(seed)
    ks = jax.random.split(key, 24)
    f32 = jnp.float32
    nrm = lambda k_, shape, scale: jax.random.normal(k_, shape, f32) * scale
    d = D_MODEL
    x = jax.random.normal(ks[0], (BATCH, SEQ, d), f32)
    c = jax.random.normal(ks[1], (BATCH, d), f32)
    offset = jax.random.randint(ks[2], (BATCH, 1), 0, 1024, dtype=jnp.int32)
    positions = offset + jnp.arange(SEQ, dtype=jnp.int32)[None, :]
    return {
        "x": x,
        "c": c,
        "positions": positions,
        "w_ada": nrm(ks[3], (DEPTH, d, 6 * d), 0.5 * d ** -0.5),
        "b_ada": nrm(ks[4], (DEPTH, 6 * d), 0.02),
        "norm1_g": 1.0 + nrm(ks[5], (DEPTH, d), 0.02),
        "w_in": nrm(ks[6], (DEPTH, d, IN_COLS), d ** -0.5),
        "gmlp_ln_g": 1.0 + nrm(ks[7], (DEPTH, A_WIDTH), 0.02),
        "gmlp_ln_b": nrm(ks[8], (DEPTH, A_WIDTH), 0.02),
        "gmlp_w_s": nrm(ks[9], (DEPTH, A_GROUPS, CHUNK, CHUNK), 0.5 * CHUNK ** -0.5),
        "gmlp_b_s": 1.0 + nrm(ks[10], (DEPTH, A_GROUPS, CHUNK), 0.02),
        "idx_k_ln_g": 1.0 + nrm(ks[11], (DEPTH, IDX_DIM), 0.02),
        "idx_k_ln_b": nrm(ks[12], (DEPTH, IDX_DIM), 0.02),
        "w_proj_a": nrm(ks[13], (DEPTH, A_WIDTH, d), A_WIDTH ** -0.5),
        "w_proj_b": nrm(ks[14], (DEPTH, N_HEADS * HEAD_DIM, d), (N_HEADS * HEAD_DIM) ** -0.5),
        "w_out": nrm(ks[15], (DEPTH, d, d), d ** -0.5),
        "norm2_g": 1.0 + nrm(ks[16], (DEPTH, d), 0.02),
        "w_ffn_in": nrm(ks[17], (DEPTH, d, 2 * D_FF), d ** -0.5),
        "w_ffn_out": nrm(ks[18], (DEPTH, D_FF, d), D_FF ** -0.5),
        "final_norm_g": 1.0 + nrm(ks[19], (d,), 0.02),
    }


def reference(x, c, positions, w_ada, b_ada, norm1_g, w_in, gmlp_ln_g, gmlp_ln_b,
              gmlp_w_s, gmlp_b_s, idx_k_ln_g, idx_k_ln_b, w_proj_a, w_proj_b, w_out,
              norm2_g, w_ffn_in, w_ffn_out, final_norm_g):
    for l in range(DEPTH):
        x = hybrid_layer(x, c, positions, w_ada[l], b_ada[l], norm1_g[l], w_in[l],
                         gmlp_ln_g[l], gmlp_ln_b[l], gmlp_w_s[l], gmlp_b_s[l],
                         idx_k_ln_g[l], idx_k_ln_b[l], w_proj_a[l], w_proj_b[l], w_out[l],
                         norm2_g[l], w_ffn_in[l], w_ffn_out[l])
    return rms_norm(x, final_norm_g)
```

```python
import numpy as np
from contextlib import ExitStack
import concourse.bass as bass
import concourse.mybir as mybir
from concourse.bass_utils import run_bass_kernel_spmd

F32 = mybir.dt.float32
BF16 = mybir.dt.bfloat16
I32 = mybir.dt.int32
AF = mybir.ActivationFunctionType
ALU = mybir.AluOpType

D = 1024
NB = 32
NOWN = 16
DFF = 2816
NFC = 22
EPS = 1e-6
NIT = 18
BIG = 1.0e30
TOPK = 256
MASKNEG = -30000.0
NDMA = 16
NDMA_SW = 8
CAST_DMA = True
B_RATE = 0.8
TWO_PI = 2.0 * np.pi
C1 = 6.28125
C2 = float(TWO_PI - 6.28125)
PI_LO = 3.1415925
OFF_IOTA = 0
OFF_IDENT = 128
OFF_TRIB = 256
OFF_TRIM = 384
OFF_INVQ = 512
OFF_INVI = 528
OFF_POW2 = 536
NCONST = 536 + NIT + 1


class Sched:
    def __init__(self, nc, es):
        self.nc = nc
        self.eng = dict(pe=nc.tensor, act=nc.scalar, dve=nc.vector, pool=nc.gpsimd, sp=nc.sync)
        self.sem = {k: es.enter_context(nc.semaphore("s_" + k)) for k in self.eng}
        self.cnt = {k: 0 for k in self.eng}
        self.dsem = [es.enter_context(nc.semaphore("d%d" % i)) for i in range(NDMA)]
        self.dcnt = 0
        self.dsem_sw = [es.enter_context(nc.semaphore("w%d" % i)) for i in range(NDMA_SW)]
        self.dcnt_sw = 0
        self.waited = {}
        self.lastw = {}
        self.readers = {}
        self.selfsync = {"act", "dve", "pool"}
        self.done = False
        self.rec = None

    @staticmethod
    def _is_excl(k):
        return (isinstance(k, tuple) and k[0] == "bk") or k in ("tp0", "tp1")

    def _deps(self, reads, writes):
        toks = []
        for r in reads:
            if r in self.lastw:
                t = self.lastw[r]
                toks.append(t + (self._is_excl(r),))
        for w in writes:
            if w in self.lastw:
                t = self.lastw[w]
                toks.append(t + (self._is_excl(w),))
            toks.extend(t + (False,) for t in self.readers.get(w, {}).values())
        return toks

    def _wait(self, e, toks):
        best = {}
        for tk in toks:
            sid, sem, val, src = tk[0], tk[1], tk[2], tk[3]
            ex = tk[4] if len(tk) > 4 else False
            if src == e and (e not in self.selfsync or ex):
                continue
            if sid not in best or best[sid][1] < val:
                best[sid] = (sem, val)
        for sid, (sem, val) in best.items():
            if self.waited.get((e, sid), 0) >= val:
                continue
            self.eng[e].wait_ge(sem, val)
            self.waited[(e, sid)] = val

    def _record(self, tok, reads, writes):
        for w in writes:
            self.lastw[w] = tok
            self.readers[w] = {}
        for r in reads:
            if self._is_excl(r):
                self.lastw[r] = tok
                self.readers[r] = {}
                continue
            d = self.readers.setdefault(r, {})
            if tok[0] not in d or d[tok[0]][2] < tok[2]:
                d[tok[0]] = tok

    def begin_record(self):
        self.rec = [[]]

    def cut(self):
        if self.rec is not None and self.rec[-1]:
            self.rec.append([])

    def end_record(self):
        r = [g for g in self.rec if g]
        self.rec = None
        return r

    def play(self, group):
        for (kind, a, kw) in group:
            if kind == "op":
                self.op(*a, **kw)
            else:
                self.dma(*a, **kw)

    def op(self, e, name, reads=(), writes=(), **kw):
        if self.done:
            return
        if self.rec is not None:
            self.rec[-1].append(("op", (e, name, reads, writes), kw))
            return
        self._wait(e, self._deps(reads, writes))
        ins = getattr(self.eng[e], name)(**kw)
        self.cnt[e] += 1
        ins.then_inc(self.sem[e], 1)
        tok = (e, self.sem[e], self.cnt[e], e)
        self._record(tok, reads, writes)

    def dma(self, e, out, in_, reads=(), writes=()):
        if self.done:
            return
        if self.rec is not None:
            self.rec[-1].append(("dma", (e, out, in_, reads, writes), {}))
            return
        if e == "pool":
            i = self.dcnt_sw
            self.dcnt_sw += 1
            n, sems, tag = NDMA_SW, self.dsem_sw, "w"
        else:
            i = self.dcnt
            self.dcnt += 1
            n, sems, tag = NDMA, self.dsem, "d"
        slot = i % n
        val = 16 * (i // n + 1)
        toks = self._deps(reads, writes)
        if i >= n:
            toks.append(((tag, slot), sems[slot], val - 16, "dma"))
        self._wait(e, toks)
        self.eng[e].dma_start(out=out, in_=in_).then_inc(sems[slot], 16)
        tok = ((tag, slot), sems[slot], val, "dma")
        self._record(tok, reads, writes)

    def wait_all(self, e):
        self._wait(e, list(self.lastw.values()))

    def barrier(self):
        toks = [(k, self.sem[k], self.cnt[k], "bar") for k in self.eng if self.cnt[k] > 0]
        for slot in range(NDMA):
            n = (self.dcnt - slot + NDMA - 1) // NDMA
            if n > 0:
                toks.append((("d", slot), self.dsem[slot], 16 * n, "bar"))
        for slot in range(NDMA_SW):
            n = (self.dcnt_sw - slot + NDMA_SW - 1) // NDMA_SW
            if n > 0:
                toks.append((("w", slot), self.dsem_sw[slot], 16 * n, "bar"))
        for e in self.eng:
            self._wait(e, [t for t in toks if t[0] != e])


def build(stop=None):
    nc = bass.Bass("TRN2", target_bir_lowering=False)

    def din(name, shape, dt=F32):
        return nc.dram_tensor(name, list(shape), dt, kind="ExternalInput").ap()

    x_d = din("x", [NB, 128, D])
    pos_d = din("pos", [128, NB], I32)
    cT_d = din("cT", [128, 8])
    wada_d = din("wada", [12, 128, 4096])
    badaf_d = din("badaf", [128, 48])
    badag_d = din("badag", [128, 2048])
    gfm_d = din("gfm", [128, 16])
    fng_d = din("fng", [128, D])
    lng_d = din("lng", [128, 2048])
    kln_d = din("kln", [128, 256])
    consts_d = din("consts", [128, NCONST])
    qrel_d = din("qrel", [128, NOWN])
    wkv_d = din("wkv", [128, 8 * 640])
    wq_d = din("wq", [128, 8 * 1544])
    wv_d = din("wv", [128, 8 * 1024])
    wuvg_d = din("wuvg", [24, 128, 1024])
    wpa_d = din("wpa", [8, 128, 1024])
    wpb_d = din("wpb", [8, 128, 1024])
    wout_d = din("wout", [128, 8 * 1024])
    ws_d = din("ws", [128, 1024])
    bs_d = din("bs", [1, 1024])
    wffi_d = din("wffi", [NFC, 128, 2048])
    wffo_d = din("wffo", [128, NFC * 1024])
    out_d = nc.dram_tensor("out", [NOWN, 128, D], F32, kind="ExternalOutput").ap()
    dbg_d = nc.dram_tensor("dbg", [128, 32768], F32, kind="ExternalOutput").ap() if stop else None
    NWCH = 122
    wbf_d = nc.dram_tensor("wbf", [NWCH + 13, 128, 1024], BF16, kind="Internal").ap()
    wsrc = [wuvg_d[ch] for ch in range(24)]
    wsrc += [wv_d[:, i * 1024:(i + 1) * 1024] for i in range(8)]
    for cc in range(8):
        wsrc += [wpa_d[cc], wpb_d[cc]]
    wsrc += [wout_d[:, i * 1024:(i + 1) * 1024] for i in range(8)]
    for cp in range(NFC):
        wsrc += [wffi_d[cp][:, 0:1024], wffi_d[cp][:, 1024:2048]]
    wsrc += [wffo_d[:, i * 1024:(i + 1) * 1024] for i in range(NFC)]
    assert len(wsrc) == NWCH
    WB_U, WB_V, WB_A, WB_O, WB_FI, WB_FO, WB_Q = 0, 24, 32, 48, 56, 100, 122
    wcols = [1024] * NWCH
    for i in range(0, 8 * 1544, 1024):
        wsrc.append(wq_d[:, i:min(i + 1024, 8 * 1544)])
        wcols.append(min(1024, 8 * 1544 - i))
    NWALL = len(wsrc)

    with ExitStack() as es:
        S = Sched(nc, es)

        uniq = [0]

        def sb(stack, name, cols, dt=F32, parts=128):
            uniq[0] += 1
            return stack.enter_context(nc.sbuf_tensor("sb%d_%s" % (uniq[0], name), [parts, cols], dt))

        bk = [es.enter_context(nc.psum_tensor("bk%d" % i, [128, 512], F32)) for i in range(6)]
        tp = [es.enter_context(nc.psum_tensor("tp%d" % i, [128, 1024], BF16)) for i in range(2)]

        consts = sb(es, "consts", NCONST)
        ident = sb(es, "ident", 128, BF16)
        modF = sb(es, "modF", 48)
        GS = sb(es, "GS", 32)
        gfm = sb(es, "gfm", 16)
        sc2 = sb(es, "sc2", 16)
        xs = [sb(es, "xs%d" % i, D) for i in range(2)]
        xn = sb(es, "xn", D, BF16)
        junkb = sb(es, "junkb", D, BF16)
        st = sb(es, "st", 16)
        neghalf = sb(es, "neghalf", 1)
        screp = sb(es, "screp", 1024)
        ybT = sb(es, "ybT", 8 * 2048, BF16)
        stage = []
        stage_k = [0]

        def chk(name, aps):
            if stop != name or S.done:
                return
            S.barrier()
            o = 0
            for ap, n in aps:
                for i in range(0, n, 1024):
                    m = min(1024, n - i)
                    S.op("dve", "tensor_copy", out=xs[0][:, 0:m], in_=ap[:, i:i + m], reads=["dbgx"], writes=["dbgx"])
                    S.dma("sp", out=dbg_d[:, o:o + m], in_=xs[0][:, 0:m], reads=["dbgx"], writes=["dbgx"])
                    o += m
            S.barrier()
            S.done = True

        iota_row = consts[:, OFF_IOTA:OFF_IOTA + 128]
        trib = consts[:, OFF_TRIB:OFF_TRIB + 128]
        trim = consts[:, OFF_TRIM:OFF_TRIM + 128]

        cast_rr = [0]
        cast_engs = [["pool"]]

        def cast(dst_ap, dkey, src_ap, skey):
            e = cast_engs[0][cast_rr[0] % len(cast_engs[0])]
            cast_rr[0] += 1
            if e == "act":
                S.op("act", "activation", out=dst_ap, in_=src_ap, func=AF.Copy, reads=[skey], writes=[dkey])
            else:
                S.op(e, "tensor_copy", out=dst_ap, in_=src_ap, reads=[skey], writes=[dkey])

        def load_bf(dst, dkey, base, ncols, ks=None):
            for k in (range(ncols // 1024) if ks is None else ks):
                S.dma("sp", out=dst[:, k * 1024:(k + 1) * 1024], in_=wbf_d[base + k],
                      reads=[("wbf", base + k)], writes=[(dkey, k)])

        def wkeys(dkey, c0, c1):
            return [(dkey, k) for k in range(c0 // 1024, (c1 - 1) // 1024 + 1)]

        def load_cast(dst, dkey, src, ncols):
            for i in range(0, ncols, 1024):
                n = min(1024, ncols - i)
                k = stage_k[0] % len(stage)
                stage_k[0] += 1
                S.dma("sp", out=stage[k][:, 0:n], in_=src[:, i:i + n], writes=[("stage", k)])
                cast(dst[:, i:i + n], dkey, stage[k][:, 0:n], ("stage", k))

        S.dma("sp", out=consts[:], in_=consts_d[:, :], writes=["consts"])
        S.dma("sp", out=gfm[:], in_=gfm_d[:, :], writes=["gfm"])
        S.op("dve", "tensor_copy", out=ident[:], in_=consts[:, OFF_IDENT:OFF_IDENT + 128],
             reads=["consts"], writes=["ident"])
        S.op("pool", "memset", ap=neghalf[:], constant=-0.5, writes=["neghalf"])

        with ExitStack() as es0:
            cT = sb(es0, "cT", 8)
            sc = sb(es0, "sc", 8)
            ones_f = sb(es0, "ones_f", 128)
            wa = [sb(es0, "wa%d" % i, 4096) for i in range(2)]
            badaf = sb(es0, "badaf", 48)
            S.dma("sp", out=cT[:], in_=cT_d[:, :], writes=["cT"])
            S.dma("sp", out=badaf[:], in_=badaf_d[:, :], writes=["badaf"])
            S.op("act", "activation", out=sc[:], in_=cT[:], func=AF.Silu, reads=["cT"], writes=["sc"])
            S.op("pool", "memset", ap=ones_f[:], constant=1.0, writes=["ones_f"])
            for kc in range(8):
                S.op("dve", "tensor_copy", out=sc2[:, 2 * kc:2 * kc + 2],
                     in_=sc[:, kc:kc + 1].broadcast_to([128, 2]), reads=["sc"], writes=["sc2"])
                S.op("dve", "tensor_scalar", out=screp[:, kc * 128:(kc + 1) * 128], in0=ones_f[:],
                     scalar1=sc[:, kc:kc + 1], scalar2=None, op0=ALU.mult,
                     reads=["sc", "ones_f"], writes=["screp"])
            order = [0, 1, 2, 3, 6, 7, 8, 9]
            for n_, ch in enumerate(order):
                w = wa[n_ % 2]
                wk = ("wa", n_ % 2)
                S.dma("sp", out=w[:], in_=wada_d[ch], writes=[wk])
                if True:
                    ps = bk[2 + n_ % 2]
                    pk = ("bk", 2 + n_ % 2)
                    for jj in range(4):
                        for kc in range(8):
                            S.op("pe", "matmul", out=ps[:, 2 * jj:2 * jj + 2],
                                 lhsT=w[:, kc * 512 + jj * 128:kc * 512 + (jj + 1) * 128],
                                 rhs=sc2[:, 2 * kc:2 * kc + 2], start=(kc == 0), stop=(kc == 7),
                                 reads=["sc2", wk], writes=[pk])
                    psv = ps[:, 0:8].rearrange("p (a b) -> p a b", b=2)[:, :, 0]
                    S.op("dve", "tensor_tensor", out=modF[:, ch * 4:ch * 4 + 4], in0=psv,
                         in1=badaf[:, ch * 4:ch * 4 + 4], op=ALU.add, reads=[pk, "badaf"], writes=["modF"])
            for i_, (sh, scl) in enumerate([(0, 8), (24, 32)]):
                S.op("dve", "scalar_tensor_tensor", out=GS[:, 16 * i_:16 * i_ + 8], in0=modF[:, scl:scl + 8],
                     scalar=1.0, in1=gfm[:, 8 * i_:8 * i_ + 8], op0=ALU.add, op1=ALU.mult,
                     reads=["modF", "gfm"], writes=["GS"])
                S.op("dve", "tensor_copy", out=GS[:, 16 * i_ + 8:16 * i_ + 16], in_=modF[:, sh:sh + 8],
                     reads=["modF"], writes=["GS"])
            S.barrier()
        chk("setup", [(GS[:, :], 32), (modF[:, :], 48), (screp[:, :], 1024)])

        def rstd_from_sum(col_in, col_out, scale, key_in, key_out):
            S.op("dve", "tensor_scalar", out=st[:, col_out:col_out + 1], in0=st[:, col_in:col_in + 1],
                 scalar1=scale, scalar2=EPS, op0=ALU.mult, op1=ALU.add, reads=[key_in], writes=[key_out + "_v"])
            S.op("act", "activation", out=st[:, col_out:col_out + 1], in_=st[:, col_out:col_out + 1], func=AF.Sqrt,
                 reads=[key_out + "_v"], writes=[key_out + "_s"])
            S.op("dve", "reciprocal", out=st[:, col_out:col_out + 1], in_=st[:, col_out:col_out + 1],
                 reads=[key_out + "_s"], writes=[key_out])

        def norm_block(xap, xkey, gofs, hT, hkey, hstride, hofs):
            S.op("act", "activation", out=junkb[:], in_=xap, func=AF.Square, accum_out=st[:, 0:1],
                 reads=[xkey], writes=["junkb", "ss"])
            rstd_from_sum(0, 1, 1.0 / D, "ss", "rstd")
            S.op("act", "activation", out=xn[:], in_=xap, func=AF.Copy, scale=st[:, 1:2],
                 reads=[xkey, "rstd"], writes=["xn"])
            for kc in range(8):
                S.op("pe", "transpose", out=tp[0][:, kc * 128:(kc + 1) * 128], in_=xn[:, kc * 128:(kc + 1) * 128],
                     identity=ident[:], reads=["xn", "ident"], writes=["tp0"])
            for kc in range(8):
                S.op("dve", "tensor_scalar", reads=["tp0", "GS"], writes=[(hkey, kc)],
                     out=hT[:, kc * hstride + hofs:kc * hstride + hofs + 128],
                     in0=tp[0][:, kc * 128:(kc + 1) * 128],
                     scalar1=GS[:, gofs + kc:gofs + kc + 1], scalar2=GS[:, gofs + 8 + kc:gofs + 9 + kc],
                     op0=ALU.mult, op1=ALU.add)

        def rope(src, skey, dst, dkey, nh, hd, rd, cs, sn, tmp):
            S.op("act", "activation", out=dst, in_=src, func=AF.Copy, reads=[skey], writes=[dkey])
            h2 = rd // 2
            s3 = src.rearrange("p (h d) -> p h d", d=hd)
            d3 = dst.rearrange("p (h d) -> p h d", d=hd)
            x1, x2 = s3[:, :, 0:h2], s3[:, :, h2:rd]
            cb = cs.unsqueeze(1).broadcast_to([128, nh, h2])
            sbb = sn.unsqueeze(1).broadcast_to([128, nh, h2])
            t = [tmp[:, i * nh * h2:(i + 1) * nh * h2].rearrange("p (h d) -> p h d", d=h2) for i in range(4)]
            S.op("dve", "tensor_tensor", out=t[0], in0=x1, in1=cb, op=ALU.mult, reads=[skey, "ropetab"], writes=["rt0"])
            S.op("dve", "tensor_tensor", out=t[1], in0=x2, in1=sbb, op=ALU.mult, reads=[skey, "ropetab"], writes=["rt1"])
            S.op("dve", "tensor_tensor", out=t[2], in0=x2, in1=cb, op=ALU.mult, reads=[skey, "ropetab"], writes=["rt2"])
            S.op("dve", "tensor_tensor", out=t[3], in0=x1, in1=sbb, op=ALU.mult, reads=[skey, "ropetab"], writes=["rt3"])
            S.op("dve", "tensor_tensor", out=d3[:, :, 0:h2], in0=t[0], in1=t[1], op=ALU.subtract,
                 reads=["rt0", "rt1", dkey], writes=[dkey])
            S.op("dve", "tensor_tensor", out=d3[:, :, h2:rd], in0=t[2], in1=t[3], op=ALU.add,
                 reads=["rt2", "rt3", dkey], writes=[dkey])

        def record(fn, *a):
            S.begin_record()
            fn(*a)
            return S.end_record()

        def interleave(lists, weights=None, rates=None):
            if weights is None:
                weights = [[1.0] * len(l) for l in lists]
            idx = [0] * len(lists)
            cum = [0.0] * len(lists)
            tot = [max(1e-9, sum(w)) for w in weights]
            while True:
                cand = [i for i in range(len(lists)) if idx[i] < len(lists[i])]
                if not cand:
                    break
                i = min(cand, key=lambda i_: cum[i_] / tot[i_] / (rates[i_] if rates else 1.0))
                S.play(lists[i][idx[i]])
                cum[i] += weights[i][idx[i]]
                idx[i] += 1


        with ExitStack() as es1:
            KT = sb(es1, "KT", 2 * 4096, BF16)
            VA = sb(es1, "VA", NB * 2 * 132, BF16)
            kiT = sb(es1, "kiT", 4096, BF16)
            cosq = sb(es1, "cosq", NB * 16)
            sinq = sb(es1, "sinq", NB * 16)
            cosi = sb(es1, "cosi", NB * 8)
            sini = sb(es1, "sini", NB * 8)
            rtmp = sb(es1, "rtmp", 4 * 8 * 16)
            hTb = sb(es1, "hTb", 1024, BF16)
            kln = sb(es1, "kln", 256)
            S.dma("sp", out=kln[:], in_=kln_d[:, :], writes=["kln"])
            S.op("pool", "memset", ap=VA[:], constant=1.0, writes=["VA"])

            with ExitStack() as est:
                posi = sb(est, "posi", NB, I32)
                posf = sb(est, "posf", NB)
                ang = sb(est, "ang", NB * 16)
                nf = sb(est, "nf", NB * 16)
                ni = sb(est, "ni", NB * 16, I32)
                S.dma("sp", out=posi[:], in_=pos_d[:, :], writes=["posi"])
                S.op("dve", "tensor_copy", out=posf[:], in_=posi[:], reads=["posi"], writes=["posf"])
                for (nfreq, off, ctab, stab) in [(16, OFF_INVQ, cosq, sinq), (8, OFF_INVI, cosi, sini)]:
                    n_ = NB * nfreq
                    a3 = ang[:, 0:n_].rearrange("p (b i) -> p b i", i=nfreq)
                    S.op("dve", "tensor_tensor", out=a3, in0=posf[:].unsqueeze(2).broadcast_to([128, NB, nfreq]),
                         in1=consts[:, off:off + nfreq].unsqueeze(1).broadcast_to([128, NB, nfreq]), op=ALU.mult,
                         reads=["posf", "consts", "ang"], writes=["ang"])
                    for (shift, tab) in [(0.0, stab), (np.pi / 2, ctab)]:
                        S.op("dve", "tensor_scalar", out=nf[:, 0:n_], in0=ang[:, 0:n_], scalar1=shift,
                             scalar2=1.0 / TWO_PI, op0=ALU.add, op1=ALU.mult, reads=["ang", "nf"], writes=["nf"])
                        S.op("dve", "tensor_copy", out=ni[:, 0:n_], in_=nf[:, 0:n_], reads=["nf", "ni"], writes=["ni"])
                        S.op("dve", "tensor_copy", out=nf[:, 0:n_], in_=ni[:, 0:n_], reads=["ni"], writes=["nf"])
                        S.op("dve", "scalar_tensor_tensor", out=tab[:], in0=nf[:, 0:n_], scalar=-C1, in1=ang[:, 0:n_],
                             op0=ALU.mult, op1=ALU.add, reads=["nf", "ang"], writes=["ropetab"])
                        S.op("dve", "scalar_tensor_tensor", out=tab[:], in0=nf[:, 0:n_], scalar=-C2, in1=tab[:],
                             op0=ALU.mult, op1=ALU.add, reads=["nf", "ropetab"], writes=["ropetab"])
                        S.op("dve", "tensor_scalar", out=tab[:], in0=tab[:], scalar1=shift, scalar2=PI_LO,
                             op0=ALU.add, op1=ALU.min, reads=["ropetab"], writes=["ropetab"])
                        S.op("dve", "tensor_scalar", out=tab[:], in0=tab[:], scalar1=-PI_LO, scalar2=None,
                             op0=ALU.max, reads=["ropetab"], writes=["ropetab"])
                        S.op("act", "activation", out=tab[:], in_=tab[:], func=AF.Sin, reads=["ropetab"], writes=["ropetab"])
                S.barrier()

            chk("tables", [(cosq[:, :], 512), (sinq[:, :], 512), (cosi[:, :], 256), (sini[:, :], 256)])
            with ExitStack() as esp1:
                stage[:] = [sb(esp1, "stage%d" % i, 1024) for i in range(2)]
                wkv = sb(esp1, "wkv", 8 * 640, BF16)
                k_r = sb(esp1, "k_r", 256, BF16)
                kif = sb(esp1, "kif", 128)
                ki_r = sb(esp1, "ki_r", 128, BF16)
                bst = sb(esp1, "bst", 8)
                load_cast(wkv, "wkv", wkv_d, 8 * 640)
                stage2 = [sb(esp1, "cstage%d" % i, 1024) for i in range(2)]
                cbuf = [sb(esp1, "cbuf%d" % i, 1024, BF16) for i in range(2)]

                def conv_steps(i0, i1):
                    for i in range(i0, i1):
                        n = wcols[i]
                        if CAST_DMA:
                            S.dma("pool", out=wbf_d[i][:, 0:n], in_=wsrc[i], writes=[("wbf", i)])
                            continue
                        if i == 0:
                            S.dma("sp", out=stage2[0][:, 0:wcols[0]], in_=wsrc[0], writes=[("cstage", 0)])
                        if i + 1 < NWALL:
                            S.dma("sp", out=stage2[(i + 1) % 2][:, 0:wcols[i + 1]], in_=wsrc[i + 1],
                                  writes=[("cstage", (i + 1) % 2)])
                        if i % 3 == 2:
                            S.op("act", "activation", out=cbuf[i % 2][:, 0:n], in_=stage2[i % 2][:, 0:n], func=AF.Copy,
                                 reads=[("cstage", i % 2)], writes=[("cbuf", i % 2)])
                        else:
                            S.op("pool", "tensor_copy", out=cbuf[i % 2][:, 0:n], in_=stage2[i % 2][:, 0:n],
                                 reads=[("cstage", i % 2)], writes=[("cbuf", i % 2)])
                        S.dma("sp", out=wbf_d[i][:, 0:n], in_=cbuf[i % 2][:, 0:n], reads=[("cbuf", i % 2)],
                              writes=[("wbf", i)])

                hTb2 = [hTb, sb(esp1, "hTb_b", 1024, BF16)]
                S.dma("sp", out=xs[0][:], in_=x_d[0], writes=[("xs", 0)])

                def p1a(blk):
                    xb = xs[blk % 2]
                    xk = ("xs", blk % 2)
                    hT_, hk_ = hTb2[blk % 2], "hTb%d" % (blk % 2)
                    if blk + 1 < NB:
                        S.dma("sp", out=xs[(blk + 1) % 2][:], in_=x_d[blk + 1], writes=[("xs", (blk + 1) % 2)])
                    if blk < 13:
                        conv_steps(WB_Q + blk, WB_Q + blk + 1)
                    norm_block(xb[:], xk, 0, hT_, hk_, 128, 0)
                    S.cut()
                    pa, pb = bk[2 * (blk % 2)], bk[2 * (blk % 2) + 1]
                    ka, kb_ = ("bk", 2 * (blk % 2)), ("bk", 2 * (blk % 2) + 1)
                    for kc in range(8):
                        S.op("pe", "matmul", out=pa[:, 0:512], lhsT=hT_[:, kc * 128:(kc + 1) * 128],
                             rhs=wkv[:, kc * 640:kc * 640 + 512], start=(kc == 0), stop=(kc == 7),
                             reads=[(hk_, kc), "wkv"], writes=[ka])
                    for kc in range(8):
                        S.op("pe", "matmul", out=pb[:, 0:128], lhsT=hT_[:, kc * 128:(kc + 1) * 128],
                             rhs=wkv[:, kc * 640 + 512:kc * 640 + 640], start=(kc == 0), stop=(kc == 7),
                             reads=[(hk_, kc), "wkv"], writes=[kb_])
                    S.cut()

                def p1b(blk):
                    pa, pb = bk[2 * (blk % 2)], bk[2 * (blk % 2) + 1]
                    ka, kb_ = ("bk", 2 * (blk % 2)), ("bk", 2 * (blk % 2) + 1)
                    rope(pa[:, 0:256], ka, k_r[:], "k_r", 2, 128, 32, cosq[:, blk * 16:(blk + 1) * 16],
                         sinq[:, blk * 16:(blk + 1) * 16], rtmp)
                    S.cut()
                    for g in range(2):
                        S.op("pe", "transpose", out=tp[1][:, g * 128:(g + 1) * 128], in_=k_r[:, g * 128:(g + 1) * 128],
                             identity=ident[:], reads=["k_r", "ident"], writes=["tp1"])
                    S.op("dve", "tensor_copy", out=KT[:].rearrange("p (g s) -> p g s", g=2)[:, :, blk * 128:(blk + 1) * 128],
                         in_=tp[1][:, 0:256].rearrange("p (g s) -> p g s", g=2),
                         reads=["tp1"], writes=["KT"])
                    S.cut()
                    S.op("act", "activation",
                         out=VA[:, blk * 264:(blk + 1) * 264].rearrange("p (g d) -> p g d", g=2)[:, :, 0:128],
                         in_=pa[:, 256:512].rearrange("p (g d) -> p g d", g=2), func=AF.Copy,
                         reads=[ka], writes=["VA"])
                    S.op("dve", "bn_stats", out=bst[:, 0:6], in_=pb[:, 0:64], reads=[kb_], writes=["bst"])
                    S.op("dve", "bn_aggr", out=st[:, 4:6], in_=bst[:, 0:6], reads=["bst"], writes=["kmv"])
                    S.op("dve", "tensor_scalar", out=st[:, 6:7], in0=st[:, 5:6], scalar1=EPS, scalar2=None,
                         op0=ALU.add, reads=["kmv"], writes=["kv_v"])
                    S.op("act", "activation", out=st[:, 6:7], in_=st[:, 6:7], func=AF.Sqrt,
                         reads=["kv_v"], writes=["kv_s"])
                    S.op("dve", "reciprocal", out=st[:, 6:7], in_=st[:, 6:7], reads=["kv_s"], writes=["krstd"])
                    S.cut()
                    S.op("dve", "tensor_scalar", out=kif[:], in0=pb[:, 0:128], scalar1=st[:, 4:5], scalar2=st[:, 6:7],
                         op0=ALU.subtract, op1=ALU.mult, reads=[kb_, "kmv", "krstd"], writes=["kif"])
                    S.op("dve", "tensor_tensor", out=kif[:], in0=kif[:], in1=kln[:, 0:128], op=ALU.mult,
                         reads=["kif", "kln"], writes=["kif"])
                    S.op("dve", "tensor_tensor", out=kif[:], in0=kif[:], in1=kln[:, 128:256], op=ALU.add,
                         reads=["kif", "kln"], writes=["kif"])
                    S.cut()
                    rope(kif[:], "kif", ki_r[:], "ki_r", 2, 64, 16, cosi[:, blk * 8:(blk + 1) * 8],
                         sini[:, blk * 8:(blk + 1) * 8], rtmp)
                    S.cut()
                    S.op("pe", "transpose", out=tp[1][:, 256:384], in_=ki_r[:], identity=ident[:],
                         reads=["ki_r", "ident"], writes=["tp1"])
                    S.op("dve", "tensor_copy", out=kiT[:, blk * 128:(blk + 1) * 128], in_=tp[1][:, 256:384],
                         reads=["tp1"], writes=["kiT"])
                    S.cut()

                for r in range(NB + 1):
                    ls = []
                    if 0 <= r - 1 < NB:
                        ls.append(record(p1b, r - 1))
                    if r < NB:
                        ls.append(record(p1a, r))
                    interleave(ls)
                S.barrier()

            chk("p1", [(KT[:, :], 8192), (kiT[:, :], 4096), (VA[:, :], NB * 264), (cosq[:, :], 512), (sinq[:, :], 512), (cosi[:, :], 256), (sini[:, :], 256)])
            with ExitStack() as esp2:
                wq = sb(esp2, "wq", 8 * 1544, BF16)
                score2 = [sb(esp2, "score%d" % i, 4096) for i in range(2)]
                mask = sb(esp2, "mask", 4096, BF16)
                maskT = [sb(esp2, "maskT%d" % i, 4096, BF16) for i in range(2)]
                rbuf = [sb(esp2, "rbuf%d" % i, 512) for i in range(2)]
                ebuf = [sb(esp2, "ebuf%d" % i, 512, BF16) for i in range(2)]
                ptb = [sb(esp2, "ptb%d" % i, 512, BF16) for i in range(2)]
                q_r = sb(esp2, "q_r", 1024, BF16)
                qi_r = sb(esp2, "qi_r", 512, BF16)
                qT = [sb(esp2, "qT%d" % i, 1024, BF16) for i in range(3)]
                qiT = sb(esp2, "qiT", 1024, BF16)
                S.op("pool", "memset", ap=qiT[:], constant=0.0, writes=["qiT"])
                y_b = sb(esp2, "y_b", 1024, BF16)
                wab = sb(esp2, "wab", 8)
                wsg = sb(esp2, "wsg", 8)
                qrel = sb(esp2, "qrel", NOWN)
                halfs2 = [sb(esp2, "halfs%d" % i, NIT + 1) for i in range(2)]
                bisA2 = [sb(esp2, "bisA%d" % i, 2) for i in range(2)]
                bis = sb(esp2, "bis", 8)
                obias = sb(esp2, "obias", 128)
                rc = sb(esp2, "rc", 4)
                S.dma("sp", out=qrel[:], in_=qrel_d[:, :], writes=["qrel"])
                for k_ in range(13):
                    n_ = wcols[WB_Q + k_]
                    S.dma("sp", out=wq[:, k_ * 1024:k_ * 1024 + n_], in_=wbf_d[WB_Q + k_][:, 0:n_],
                          reads=[("wbf", WB_Q + k_)], writes=[("wq", k_)])
                mk3 = mask[:].rearrange("p (a s) -> p a s", a=2)
                c0 = float(64 ** -0.5 * 8 ** -0.5)
                att_scale = float(128 ** -0.5)

                def stage_A(j):
                    L = 128 * (j + 1)
                    xb = xs[j % 2]
                    xk = ("xs", j % 2)
                    qTj, qTk = qT[j % 3], ("qT", j % 3)
                    score, SK = score2[j % 2], ("scr", j % 2)
                    sc3 = score[:].rearrange("p (a s) -> p a s", a=2)
                    halfs, HK = halfs2[j % 2], ("halfs", j % 2)
                    bisA = bisA2[j % 2]
                    S.dma("sp", out=xb[:], in_=x_d[j], writes=[xk])
                    norm_block(xb[:], xk, 0, hTb, "hTb", 128, 0)
                    S.cut()
                    for n_ in range(2):
                        for kc in range(8):
                            S.op("pe", "matmul", out=bk[n_][:, 0:512], lhsT=hTb[:, kc * 128:(kc + 1) * 128],
                                 rhs=wq[:, kc * 1544 + n_ * 512:kc * 1544 + (n_ + 1) * 512], start=(kc == 0), stop=(kc == 7),
                                 reads=[("hTb", kc)] + wkeys("wq", kc * 1544 + n_ * 512, kc * 1544 + (n_ + 1) * 512),
                                 writes=[("bk", n_)])
                    S.cut()
                    cq, sq_ = cosq[:, j * 16:(j + 1) * 16], sinq[:, j * 16:(j + 1) * 16]
                    rope(bk[0][:, 0:512], ("bk", 0), q_r[:, 0:512], "q_r0", 4, 128, 32, cq, sq_, rtmp)
                    S.cut()
                    rope(bk[1][:, 0:512], ("bk", 1), q_r[:, 512:1024], "q_r1", 4, 128, 32, cq, sq_, rtmp)
                    S.cut()
                    for kc in range(8):
                        S.op("pe", "matmul", out=bk[0][:, 0:512], lhsT=hTb[:, kc * 128:(kc + 1) * 128],
                             rhs=wq[:, kc * 1544 + 1024:kc * 1544 + 1536], start=(kc == 0), stop=(kc == 7),
                             reads=[("hTb", kc)] + wkeys("wq", kc * 1544 + 1024, kc * 1544 + 1536), writes=[("bk", 0)])
                    for kc in range(8):
                        S.op("pe", "matmul", out=bk[1][:, 0:8], lhsT=hTb[:, kc * 128:(kc + 1) * 128],
                             rhs=wq[:, kc * 1544 + 1536:kc * 1544 + 1544], start=(kc == 0), stop=(kc == 7),
                             reads=[("hTb", kc)] + wkeys("wq", kc * 1544 + 1536, kc * 1544 + 1544), writes=[("bk", 1)])
                    S.cut()
                    rope(bk[0][:, 0:512], ("bk", 0), qi_r[:], "qi_r", 8, 64, 16, cosi[:, j * 8:(j + 1) * 8],
                         sini[:, j * 8:(j + 1) * 8], rtmp)
                    S.op("act", "activation", out=wab[:], in_=bk[1][:, 0:8], func=AF.Abs, scale=c0,
                         reads=[("bk", 1)], writes=["wab"])
                    S.op("act", "activation", out=wsg[:], in_=bk[1][:, 0:8], func=AF.Sign,
                         reads=[("bk", 1)], writes=["wsg"])
                    S.cut()
                    for h in range(8):
                        S.op("pe", "transpose", out=tp[0][:, h * 128:(h + 1) * 128], in_=q_r[:, h * 128:(h + 1) * 128],
                             identity=ident[:], reads=["q_r0", "q_r1", "ident"], writes=["tp0"])
                    S.op("dve", "tensor_copy", out=qTj[:], in_=tp[0][:], reads=["tp0"], writes=[qTk])
                    for h2 in range(4):
                        S.op("pe", "transpose", out=tp[0][:, h2 * 128:(h2 + 1) * 128], in_=qi_r[:, h2 * 128:(h2 + 1) * 128],
                             identity=ident[:], reads=["qi_r", "ident"], writes=["tp0"])
                    qz = qiT[:].rearrange("p (h t) -> p h t", h=8)
                    t3 = tp[0][:, 0:512].rearrange("p (h t) -> p h t", h=4)
                    S.op("dve", "tensor_copy", out=qz[0:64, 0:8:2, :], in_=t3[0:64, :, :], reads=["tp0"], writes=["qiT"])
                    S.op("dve", "tensor_copy", out=qz[64:128, 1:8:2, :], in_=t3[64:128, :, :], reads=["tp0"], writes=["qiT"])
                    S.cut()
                    chunks = []
                    for a in range(2):
                        for c_ in range(0, L, 512):
                            chunks.append((a, c_, min(512, L - c_)))
                    ci = 0
                    for (a, c_, n_) in chunks:
                        ko = a * 2048 + c_
                        for h in range(8):
                            pl = bk[ci % 2]
                            plk = ("bk", ci % 2)
                            rb = rbuf[ci % 2]
                            rk = ("rbuf", ci % 2)
                            ci += 1
                            r0 = (h % 2) * 64
                            S.op("pe", "matmul", out=pl[:, 0:n_], lhsT=qiT[:, h * 128:(h + 1) * 128],
                                 rhs=kiT[:, ko:ko + n_], start=True, stop=True,
                                 reads=["qiT", "kiT"], writes=[plk])
                            S.op("act", "activation", out=rb[:, 0:n_], in_=pl[:, 0:n_], func=AF.Relu,
                                 scale=wab[:, h:h + 1], reads=[plk, "wab"], writes=[rk])
                            if h == 0:
                                S.op("dve", "tensor_scalar", out=score[:, ko:ko + n_], in0=rb[:, 0:n_],
                                     scalar1=wsg[:, 0:1], scalar2=None, op0=ALU.mult,
                                     reads=[rk, "wsg"], writes=[SK])
                            else:
                                S.op("dve", "scalar_tensor_tensor", out=score[:, ko:ko + n_], in0=rb[:, 0:n_],
                                     scalar=wsg[:, h:h + 1], in1=score[:, ko:ko + n_], op0=ALU.mult, op1=ALU.add,
                                     reads=[rk, "wsg", SK], writes=[SK])
                            S.cut()
                    S.op("dve", "tensor_reduce", out=bisA[:, 0:1], in_=sc3[:, :, 0:L], axis=mybir.AxisListType.XY,
                         op=ALU.max, apply_absolute_value=True, reads=[SK], writes=[("amax", j % 2)])
                    S.op("dve", "tensor_scalar", out=bisA[:, 1:2], in0=bisA[:, 0:1], scalar1=1.001, scalar2=1e-6,
                         op0=ALU.mult, op1=ALU.add, reads=[("amax", j % 2)], writes=[("A", j % 2)])
                    S.op("dve", "tensor_scalar", out=halfs[:], in0=consts[:, OFF_POW2:OFF_POW2 + NIT + 1],
                         scalar1=bisA[:, 1:2], scalar2=None, op0=ALU.mult, reads=[("A", j % 2), "consts"], writes=[HK])
                    S.op("dve", "tensor_tensor", out=score[:, j * 128:(j + 1) * 128], in0=score[:, j * 128:(j + 1) * 128],
                         in1=trib, op=ALU.add, reads=[SK, "consts"], writes=[SK])
                    S.op("dve", "tensor_scalar", out=obias[:], in0=iota_row, scalar1=qrel[:, j:j + 1], scalar2=-BIG,
                         op0=ALU.is_gt, op1=ALU.mult, reads=["consts", "qrel"], writes=["obias"])
                    S.op("dve", "tensor_tensor", out=score[:, 2048 + j * 128:2048 + (j + 1) * 128],
                         in0=score[:, 2048 + j * 128:2048 + (j + 1) * 128], in1=obias[:], op=ALU.add,
                         reads=[SK, "obias"], writes=[SK])
                    S.cut()

                def stage_B(j):
                    L = 128 * (j + 1)
                    mTj, mTk = maskT[j % 2], ("maskT", j % 2)
                    score, SK = score2[j % 2], ("scr", j % 2)
                    sc3 = score[:].rearrange("p (a s) -> p a s", a=2)
                    halfs, HK = halfs2[j % 2], ("halfs", j % 2)
                    S.op("dve", "memset", ap=bis[:, 2:3], constant=0.0, writes=["mid"])
                    for k in range(NIT):
                        S.op("dve", "tensor_scalar", out=mask[:, 0:L], in0=score[:, 0:L], scalar1=bis[:, 2:3],
                             scalar2=None, op0=ALU.is_ge, op1=ALU.add, accum_out=bis[:, 3:4],
                             reads=[SK, "mid"], writes=["maskA", "cnt"])
                        S.op("act", "activation", out=mask[:, 2048:2048 + L], in_=score[:, 2048:2048 + L], func=AF.Sign,
                             scale=-1.0, bias=bis[:, 2:3], accum_out=bis[:, 5:6],
                             reads=[SK, "mid"], writes=["maskB", "negs"])
                        S.op("dve", "scalar_tensor_tensor", out=bis[:, 6:7], in0=bis[:, 5:6], scalar=-0.5,
                             in1=bis[:, 3:4], op0=ALU.mult, op1=ALU.add, reads=["negs", "cnt"], writes=["cnt2"])
                        S.op("dve", "scalar_tensor_tensor", out=bis[:, 4:5], in0=bis[:, 6:7], scalar=TOPK - 0.5 - L / 2.0,
                             in1=halfs[:, k:k + 1], op0=ALU.is_ge, op1=ALU.mult, reads=["cnt2", HK], writes=["btmp"])
                        S.op("dve", "scalar_tensor_tensor", out=bis[:, 2:3], in0=bis[:, 4:5], scalar=halfs[:, k + 1:k + 2],
                             in1=bis[:, 2:3], op0=ALU.subtract, op1=ALU.add, reads=["btmp", HK, "mid"], writes=["mid"])
                        S.cut()
                    S.op("dve", "tensor_scalar", out=mk3[:, :, 0:L], in0=sc3[:, :, 0:L], scalar1=bis[:, 2:3],
                         scalar2=MASKNEG, op0=ALU.is_lt, op1=ALU.mult, reads=[SK, "mid"], writes=["maskA", "maskB"])
                    S.cut()
                    kbs = list(range(j + 1)) + list(range(16, 16 + j + 1))
                    for i0 in range(0, len(kbs), 8):
                        grp = kbs[i0:i0 + 8]
                        for ii, kb in enumerate(grp):
                            S.op("pe", "transpose", out=tp[0][:, ii * 128:(ii + 1) * 128], in_=mask[:, kb * 128:(kb + 1) * 128],
                                 identity=ident[:], reads=["maskA", "maskB", "ident"], writes=["tp0"])
                        runs = []
                        for ii, kb in enumerate(grp):
                            if runs and runs[-1][1] + runs[-1][2] == kb:
                                runs[-1][2] += 1
                            else:
                                runs.append([ii, kb, 1])
                        for (ii, kb, cnt_) in runs:
                            S.op("dve", "tensor_copy", reads=["tp0"], writes=[mTk],
                                 out=mTj[:, kb * 128:(kb + cnt_) * 128], in_=tp[0][:, ii * 128:(ii + cnt_) * 128])
                        S.cut()

                def stage_C(j):
                    qTj, qTk = qT[j % 3], ("qT", j % 3)
                    mTj, mTk = maskT[j % 2], ("maskT", j % 2)
                    kbs = list(range(j + 1)) + list(range(16, 16 + j + 1))
                    ai = 0
                    for g in range(2):
                        ob = [bk[2], bk[3]]
                        for idx, kb in enumerate(kbs):
                            ps_ = bk[4 + ai % 2]
                            pk = ("bk", 4 + ai % 2)
                            eb, ek = ebuf[ai % 2], ("ebuf", ai % 2)
                            pt_, ptk = ptb[ai % 2], ("ptb", ai % 2)
                            ai += 1
                            S.op("pe", "matmul", out=ps_[:, 0:512], lhsT=KT[:, g * 4096 + kb * 128:g * 4096 + (kb + 1) * 128],
                                 rhs=qTj[:, g * 512:(g + 1) * 512], start=True, stop=False,
                                 reads=["KT", qTk], writes=[pk])
                            for h in range(4):
                                S.op("pe", "matmul", out=ps_[:, h * 128:(h + 1) * 128], lhsT=ident[:],
                                     rhs=mTj[:, kb * 128:(kb + 1) * 128], start=False, stop=(h == 3),
                                     reads=["ident", mTk], writes=[pk])
                            S.op("act", "activation", out=pt_[:], in_=ps_[:, 0:512], func=AF.Exp, scale=att_scale,
                                 reads=[pk], writes=[ptk])
                            for h in range(4):
                                S.op("pe", "matmul", out=ob[h // 2][:, (h % 2) * 256:(h % 2) * 256 + 129],
                                     lhsT=pt_[:, h * 128:(h + 1) * 128],
                                     rhs=VA[:, (kb * 2 + g) * 132:(kb * 2 + g) * 132 + 129],
                                     start=(idx == 0 and h % 2 == 0), stop=(idx == len(kbs) - 1),
                                     skip_group_check=True, reads=[ptk, "VA"], writes=[("bk", 2 + h // 2)])
                            S.cut()
                        for hh in range(2):
                            S.op("dve", "reciprocal", out=rc[:, 2 * hh:2 * hh + 2],
                                 in_=ob[hh][:, 0:512].rearrange("p (a d) -> p a d", a=2)[:, :, 128],
                                 reads=[("bk", 2 + hh)], writes=["rc"])
                        for h in range(4):
                            S.op("act", "activation", out=y_b[:, (g * 4 + h) * 128:(g * 4 + h + 1) * 128],
                                 in_=ob[h // 2][:, (h % 2) * 256:(h % 2) * 256 + 128], func=AF.Copy,
                                 scale=rc[:, h:h + 1], reads=[("bk", 2 + h // 2), "rc"], writes=["y_b"])
                        S.cut()
                    for h in range(8):
                        S.op("pe", "transpose", out=tp[1][:, h * 128:(h + 1) * 128], in_=y_b[:, h * 128:(h + 1) * 128],
                             identity=ident[:], reads=["y_b", "ident"], writes=["tp1"])
                    S.op("dve", "tensor_copy", out=ybT[:].rearrange("p (c t) -> p c t", c=8)[:, :, j * 128:(j + 1) * 128],
                         in_=tp[1][:].rearrange("p (c t) -> p c t", c=8), reads=["tp1"], writes=["ybT"])
                    S.cut()

                def conv2(i0, i1):
                    for i in range(i0, i1):
                        S.dma("pool", out=wbf_d[i], in_=wsrc[i], writes=[("wbf", i)])
                        S.cut()

                for r in range(NOWN + 2):
                    ls, rt = [], []
                    if r < NOWN:
                        ls.append(record(conv2, (NWCH * r) // NOWN, (NWCH * (r + 1)) // NOWN))
                        rt.append(1.0)
                    if 0 <= r - 2 < NOWN:
                        ls.append(record(stage_C, r - 2))
                        rt.append(1.0)
                    if 0 <= r - 1 < NOWN:
                        ls.append(record(stage_B, r - 1))
                        rt.append(B_RATE)
                    if r < NOWN:
                        ls.append(record(stage_A, r))
                        rt.append(1.0)
                    interleave(ls, None, rt)
                S.barrier()

        chk("p2a", [(ybT[:, :], 16384)])
        with ExitStack() as es3:
            cast_engs[0] = ["dve", "act"]
            stage[:] = []
            NWST, NWFI = 6, 4
            xq = sb(es3, "xq", 4 * D)
            tmpa = sb(es3, "tmpa", 512)
            tmpb = sb(es3, "tmpb", 512)
            onesz = sb(es3, "onesz", 128, BF16)
            bs2 = sb(es3, "bs2", 1024, BF16)
            S.op("pool", "memset", ap=onesz[:], constant=0.0, writes=["onesz"])
            S.op("pool", "memset", ap=onesz[0:1, :], constant=1.0, writes=["onesz"])
            S.op("pool", "memset", ap=onesz[32:33, :], constant=1.0, writes=["onesz"])
            with ExitStack() as esbs:
                bsf = sb(esbs, "bsf", 1024)
                bhf = sb(esbs, "bhf", 1024)
                blo = sb(esbs, "blo", 1024, BF16)
                S.op("pool", "memset", ap=bsf[:], constant=0.0, writes=["bsf"])
                S.dma("sp", out=bsf[0:1, :], in_=bs_d[:, :], reads=["bsf"], writes=["bsf0"])
                S.dma("sp", out=bsf[32:33, :], in_=bs_d[:, :], reads=["bsf"], writes=["bsf32"])
                S.op("dve", "tensor_copy", out=bs2[:], in_=bsf[:], reads=["bsf", "bsf0", "bsf32"], writes=["bs2"])
                S.op("dve", "tensor_copy", out=bhf[:], in_=bs2[:], reads=["bs2"], writes=["bhf"])
                S.op("dve", "tensor_tensor", out=blo[:], in0=bsf[:], in1=bhf[:], op=ALU.subtract,
                     reads=["bsf", "bsf0", "bsf32", "bhf"], writes=["blo"])
                S.op("dve", "tensor_copy", out=bs2[32:33, :], in_=blo[32:33, :], reads=["blo", "bs2"], writes=["bs2"])
                S.barrier()
            gbc = sb(es3, "gbc", 2048)
            fng = sb(es3, "fng", D)
            S.dma("sp", out=fng[:], in_=fng_d[:, :], writes=["fng"])
            S.dma("sp", out=gbc[:], in_=badag_d[:, :], writes=["gbc"])
            for n_, ch in enumerate([4, 5, 10, 11]):
                gi = 0 if ch < 6 else 1
                half = ch % 2
                ps = bk[n_ % 2]
                pk = ("bk", n_ % 2)
                for qd in range(4):
                    S.dma("sp", out=xq[:, qd * 1024:(qd + 1) * 1024], in_=wada_d[ch][:, qd * 1024:(qd + 1) * 1024],
                          writes=[("xq", qd)])
                    for k2 in range(2):
                        kc = qd * 2 + k2
                        S.op("pe", "matmul", out=ps[:, 0:512], lhsT=screp[:, kc * 128:(kc + 1) * 128],
                             rhs=xq[:, kc * 512:(kc + 1) * 512], start=(kc == 0), stop=(kc == 7),
                             reads=["screp", ("xq", qd)], writes=[pk])
                o = gi * 1024 + half * 512
                S.op("dve", "tensor_tensor", out=gbc[:, o:o + 512], in0=ps[:, 0:512],
                     in1=gbc[:, o:o + 512], op=ALU.add, reads=[pk, "gbc"], writes=["gbc"])
            for q in range(4):
                tq = q * 512
                with ExitStack() as esb:
                    hTq = sb(esb, "hTq", 8 * 512, BF16)
                    uT = sb(esb, "uT", 8 * 512, BF16)
                    sga = sb(esb, "sga", 8 * 512, BF16)
                    sgb = sb(esb, "sgb", 8 * 512, BF16)
                    yaT = sb(esb, "yaT", 8 * 512, BF16)
                    mT = sb(esb, "mT", 8 * 512, BF16)
                    w16 = sb(esb, "w16", 8 * 1024, BF16)
                    w16o = sb(esb, "w16o", 8 * 1024, BF16)
                    wst = [sb(esb, "wst%d" % i, 1024, BF16) for i in range(NWST)]
                    vg2 = [sb(esb, "vg%d" % i, 1024) for i in range(2)]
                    vg = vg2[0]
                    vnb = sb(esb, "vnb", 1024, BF16)
                    lng = sb(esb, "lng", 2048)
                    wsT = sb(esb, "wsT", 1024, BF16)
                    bst2 = sb(esb, "bst2", 12)
                    S.dma("sp", out=lng[:], in_=lng_d[:, :], writes=["lng"])
                    S.dma("sp", out=vg[:], in_=ws_d[:, :], writes=[("vg", 0)])
                    S.op("dve", "tensor_tensor", out=wsT[:].rearrange("p (g t) -> p g t", g=8),
                         in0=vg[:].rearrange("p (g t) -> p g t", g=8),
                         in1=trim.unsqueeze(1).broadcast_to([128, 8, 128]), op=ALU.mult,
                         reads=[("vg", 0), "consts"], writes=["wsT"])
                    for tb in range(4):
                        xk = ("xq", tb)
                        S.dma("sp", out=xq[:, tb * D:(tb + 1) * D], in_=x_d[q * 4 + tb], writes=[xk])
                        norm_block(xq[:, tb * D:(tb + 1) * D], xk, 0, hTq, "hTq", 512, tb * 128)
                    for ch in range(24):
                        if ch == 6:
                            load_bf(w16, "w16", WB_V, 8 * 1024)
                        if ch == 16:
                            load_bf(w16o, "w16o", WB_O, 8 * 1024)
                        k3 = ch % NWST
                        S.dma("sp", out=wst[k3][:], in_=wbf_d[WB_U + ch], reads=[("wbf", WB_U + ch)], writes=[("wst", k3)])
                        ps_ = bk[ch % 2]
                        pk = ("bk", ch % 2)
                        for kc in range(8):
                            S.op("pe", "matmul", out=ps_[:, 0:512], lhsT=wst[k3][:, kc * 128:(kc + 1) * 128],
                                 rhs=hTq[:, kc * 512:(kc + 1) * 512], start=(kc == 0), stop=(kc == 7),
                                 reads=[("wst", k3), ("hTq", kc)], writes=[pk])
                        dst, dk = [(uT, "uT"), (sga, "sga"), (sgb, "sgb")][ch // 8]
                        cc = ch % 8
                        S.op("act", "activation", out=dst[:, cc * 512:(cc + 1) * 512], in_=ps_[:, 0:512],
                             func=AF.Gelu_apprx_tanh if ch < 8 else AF.Sigmoid, reads=[pk], writes=[dk])

                    def v_mm(tb):
                        vb = [(0, 1), (2, 3)][tb % 2]
                        vgt, vgk = vg2[tb % 2], ("vg", tb % 2)
                        for n_ in range(2):
                            for kc in range(8):
                                S.op("pe", "matmul", out=bk[vb[n_]][:, 0:512],
                                     lhsT=hTq[:, kc * 512 + tb * 128:kc * 512 + (tb + 1) * 128],
                                     rhs=w16[:, kc * 1024 + n_ * 512:kc * 1024 + (n_ + 1) * 512],
                                     start=(kc == 0), stop=(kc == 7), reads=[("hTq", kc), ("w16", kc)], writes=[("bk", vb[n_])])
                            S.op("act", "activation", out=vgt[:, n_ * 512:(n_ + 1) * 512], in_=bk[vb[n_]][:, 0:512],
                                 func=AF.Gelu_apprx_tanh, reads=[("bk", vb[n_])], writes=[vgk])

                    def v_ln_mix(tb):
                        vgt, vgk = vg2[tb % 2], ("vg", tb % 2)
                        for n_ in range(2):
                            S.op("dve", "bn_stats", out=bst2[:, 6 * n_:6 * n_ + 6], in_=vgt[:, n_ * 512:(n_ + 1) * 512],
                                 reads=[vgk], writes=["bst2"])
                        S.op("dve", "bn_aggr", out=st[:, 8:10], in_=bst2[:], reads=["bst2"], writes=["vmv"])
                        S.op("dve", "tensor_scalar", out=st[:, 10:11], in0=st[:, 9:10], scalar1=EPS, scalar2=None,
                             op0=ALU.add, reads=["vmv"], writes=["vv_v"])
                        S.op("act", "activation", out=st[:, 10:11], in_=st[:, 10:11], func=AF.Sqrt,
                             reads=["vv_v"], writes=["vv_s"])
                        S.op("dve", "reciprocal", out=st[:, 10:11], in_=st[:, 10:11], reads=["vv_s"], writes=["vrstd"])
                        S.op("dve", "tensor_scalar", out=vgt[:], in0=vgt[:], scalar1=st[:, 8:9], scalar2=st[:, 10:11],
                             op0=ALU.subtract, op1=ALU.mult, reads=[vgk, "vmv", "vrstd"], writes=[vgk])
                        S.op("dve", "tensor_tensor", out=vgt[:], in0=vgt[:], in1=lng[:, 0:1024], op=ALU.mult,
                             reads=[vgk, "lng"], writes=[vgk])
                        S.op("dve", "tensor_tensor", out=vnb[:], in0=vgt[:], in1=lng[:, 1024:2048], op=ALU.add,
                             reads=[vgk, "lng"], writes=["vnb"])
                        for gh in range(2):
                            ps_ = bk[4 + gh]
                            pk = ("bk", 4 + gh)
                            for gg in range(4):
                                g = gh * 4 + gg
                                S.op("pe", "matmul", out=ps_[:, gg * 128:(gg + 1) * 128], lhsT=vnb[:, g * 128:(g + 1) * 128],
                                     rhs=wsT[:, g * 128:(g + 1) * 128], start=True, stop=False,
                                     reads=["vnb", "wsT"], writes=[pk])
                                S.op("pe", "matmul", out=ps_[:, gg * 128:(gg + 1) * 128], lhsT=onesz[:, 0:128],
                                     rhs=bs2[:, g * 128:(g + 1) * 128], start=False, stop=True,
                                     reads=["onesz", "bs2"], writes=[pk])
                            S.op("dve", "tensor_tensor",
                                 out=yaT[:].rearrange("p (c t) -> p c t", c=8)[:, gh * 4:gh * 4 + 4, tb * 128:(tb + 1) * 128],
                                 in0=ps_[:, 0:512].rearrange("p (c t) -> p c t", c=4),
                                 in1=uT[:].rearrange("p (c t) -> p c t", c=8)[:, gh * 4:gh * 4 + 4, tb * 128:(tb + 1) * 128],
                                 op=ALU.mult, reads=[pk, "uT"], writes=["yaT"])

                    v_mm(0)
                    for tb in range(4):
                        if tb + 1 < 4:
                            v_mm(tb + 1)
                        v_ln_mix(tb)

                    for cc in range(8):
                        for which, (wd, src, sk2) in enumerate([(wpa_d, yaT, "yaT"), (wpb_d, None, "ybT")]):
                            k3 = (2 * cc + which) % NWST
                            S.dma("sp", out=wst[k3][:], in_=wbf_d[WB_A + 2 * cc + which],
                                  reads=[("wbf", WB_A + 2 * cc + which)], writes=[("wst", k3)])
                            ps_ = bk[2 * (cc % 2) + which]
                            pk = ("bk", 2 * (cc % 2) + which)
                            for kc in range(8):
                                rhs = (yaT[:, kc * 512:(kc + 1) * 512] if which == 0
                                       else ybT[:, kc * 2048 + tq:kc * 2048 + tq + 512])
                                S.op("pe", "matmul", out=ps_[:, 0:512], lhsT=wst[k3][:, kc * 128:(kc + 1) * 128],
                                     rhs=rhs, start=(kc == 0), stop=(kc == 7), reads=[("wst", k3), sk2], writes=[pk])
                        S.op("dve", "tensor_tensor", out=tmpa[:], in0=bk[2 * (cc % 2)][:, 0:512], in1=sga[:, cc * 512:(cc + 1) * 512],
                             op=ALU.mult, reads=[("bk", 2 * (cc % 2)), "sga"], writes=["tmpa"])
                        S.op("dve", "tensor_tensor", out=tmpb[:], in0=bk[2 * (cc % 2) + 1][:, 0:512], in1=sgb[:, cc * 512:(cc + 1) * 512],
                             op=ALU.mult, reads=[("bk", 2 * (cc % 2) + 1), "sgb"], writes=["tmpb"])
                        S.op("dve", "tensor_tensor", out=mT[:, cc * 512:(cc + 1) * 512], in0=tmpa[:], in1=tmpb[:],
                             op=ALU.add, reads=["tmpa", "tmpb"], writes=["mT"])
                    for tb in range(4):
                        xk = ("xq", tb)
                        for n_ in range(2):
                            for kc in range(8):
                                S.op("pe", "matmul", out=bk[[4, 0][tb % 2] + n_][:, 0:512],
                                     lhsT=mT[:, kc * 512 + tb * 128:kc * 512 + (tb + 1) * 128],
                                     rhs=w16o[:, kc * 1024 + n_ * 512:kc * 1024 + (n_ + 1) * 512],
                                     start=(kc == 0), stop=(kc == 7), reads=["mT", ("w16o", kc)],
                                     writes=[("bk", [4, 0][tb % 2] + n_)])
                            t_, tk = (tmpa, "tmpa") if n_ == 0 else (tmpb, "tmpb")
                            S.op("dve", "tensor_tensor", out=t_[:], in0=bk[[4, 0][tb % 2] + n_][:, 0:512],
                                 in1=gbc[:, n_ * 512:(n_ + 1) * 512], op=ALU.mult,
                                 reads=[("bk", [4, 0][tb % 2] + n_), "gbc"], writes=[tk])
                            xo = tb * D + n_ * 512
                            S.op("dve", "tensor_tensor", out=xq[:, xo:xo + 512], in0=xq[:, xo:xo + 512], in1=t_[:],
                                 op=ALU.add, reads=[tk, xk], writes=[xk])
                    S.barrier()
                with ExitStack() as esf:
                    h2T = sb(esf, "h2T", 8 * 512, BF16)
                    aT = sb(esf, "aT", NFC * 512, BF16)
                    wffo = sb(esf, "wffo", NFC * 1024, BF16)
                    wfi = [sb(esf, "wfi%d" % i, 2048, BF16) for i in range(NWFI)]
                    sgt = [sb(esf, "sgt%d" % i, 512) for i in range(2)]
                    obuf = [sb(esf, "obuf%d" % i, D) for i in range(1)]
                    for tb in range(4):
                        norm_block(xq[:, tb * D:(tb + 1) * D], ("xq", tb), 16, h2T, "h2T", 512, tb * 128)
                    for cp in range(NFC):
                        load_bf(wfi[cp % NWFI], ("wfi", cp % NWFI), WB_FI + 2 * cp, 2048)
                        load_bf(wffo, "wffo", WB_FO, NFC * 1024, ks=[cp])
                        for half in range(2):
                            ps_ = bk[2 * (cp % 2) + half]
                            pk = ("bk", 2 * (cp % 2) + half)
                            for kc in range(8):
                                S.op("pe", "matmul", out=ps_[:, 0:512],
                                     lhsT=wfi[cp % NWFI][:, kc * 256 + half * 128:kc * 256 + (half + 1) * 128],
                                     rhs=h2T[:, kc * 512:(kc + 1) * 512], start=(kc == 0), stop=(kc == 7),
                                     reads=[(("wfi", cp % NWFI), kc // 4), ("h2T", kc)], writes=[pk])
                        S.op("act", "activation", out=sgt[cp % 2][:], in_=bk[2 * (cp % 2)][:, 0:512], func=AF.Silu,
                             reads=[("bk", 2 * (cp % 2))], writes=[("sgt", cp % 2)])
                        S.op("dve", "tensor_tensor", out=aT[:, cp * 512:(cp + 1) * 512], in0=bk[2 * (cp % 2) + 1][:, 0:512],
                             in1=sgt[cp % 2][:], op=ALU.mult, reads=[("bk", 2 * (cp % 2) + 1), ("sgt", cp % 2)],
                             writes=["aT"])
                    for tb in range(4):
                        xk = ("xq", tb)
                        for n_ in range(2):
                            for kc in range(NFC):
                                S.op("pe", "matmul", out=bk[[4, 0][tb % 2] + n_][:, 0:512],
                                     lhsT=aT[:, kc * 512 + tb * 128:kc * 512 + (tb + 1) * 128],
                                     rhs=wffo[:, kc * 1024 + n_ * 512:kc * 1024 + (n_ + 1) * 512],
                                     start=(kc == 0), stop=(kc == NFC - 1), reads=["aT", ("wffo", kc)],
                                     writes=[("bk", [4, 0][tb % 2] + n_)])
                            t_, tk = (tmpa, "tmpa") if n_ == 0 else (tmpb, "tmpb")
                            S.op("dve", "tensor_tensor", out=t_[:], in0=bk[[4, 0][tb % 2] + n_][:, 0:512],
                                 in1=gbc[:, 1024 + n_ * 512:1024 + (n_ + 1) * 512], op=ALU.mult,
                                 reads=[("bk", [4, 0][tb % 2] + n_), "gbc"], writes=[tk])
                            xo = tb * D + n_ * 512
                            S.op("dve", "tensor_tensor", out=xq[:, xo:xo + 512], in0=xq[:, xo:xo + 512], in1=t_[:],
                                 op=ALU.add, reads=[tk, xk], writes=[xk])
                        ob_, ok_ = obuf[0], ("obuf", 0)
                        S.op("act", "activation", out=junkb[:], in_=xq[:, tb * D:(tb + 1) * D], func=AF.Square,
                             accum_out=st[:, 12:13], reads=[xk], writes=["junkb", "fss"])
                        rstd_from_sum(12, 13, 1.0 / D, "fss", "frstd")
                        S.op("act", "activation", out=ob_[:], in_=xq[:, tb * D:(tb + 1) * D], func=AF.Copy,
                             scale=st[:, 13:14], reads=[xk, "frstd"], writes=[ok_])
                        S.op("dve", "tensor_tensor", out=ob_[:], in0=ob_[:], in1=fng[:], op=ALU.mult,
                             reads=[ok_, "fng"], writes=[ok_])
                        S.dma("sp", out=out_d[q * 4 + tb], in_=ob_[:], reads=[ok_], writes=[("out", q * 4 + tb)])
                    S.barrier()
        S.wait_all("sp")
    return nc


_NC = None


def _own_blocks(p):
    a = [0, 3] if p == 0 else [1, 2]
    return [4 * g + o for g in range(8) for o in a]


def _kc_layout(w, K):
    n = w.shape[1]
    return np.ascontiguousarray(w.reshape(K // 128, 128, n).transpose(1, 0, 2).reshape(128, (K // 128) * n))


def _chunked(w, K, cw):
    n = w.shape[1]
    a = w.reshape(K // 128, 128, n // cw, cw).transpose(2, 1, 0, 3)
    return np.ascontiguousarray(a.reshape(n // cw, 128, (K // 128) * cw))


def _prep(x, c, positions, w_ada, b_ada, norm1_g, w_in, gmlp_ln_g, gmlp_ln_b, gmlp_w_s, gmlp_b_s,
          idx_k_ln_g, idx_k_ln_b, w_proj_a, w_proj_b, w_out, norm2_g, w_ffn_in, w_ffn_out, final_norm_g):
    f = lambda a: np.asarray(a, dtype=np.float32)
    x = f(x); c = f(c); positions = np.asarray(positions, dtype=np.int32)
    w_ada = f(w_ada)[0]; b_ada = f(b_ada)[0]; w_in = f(w_in)[0]
    o = np.cumsum([0, 1024, 1024, 1024, 256, 256, 512, 64, 8, 1024, 1024])
    au, av, q_, k_, v_, qi_, ki_, wi_, ga_, gb_ = [w_in[:, o[i]:o[i + 1]] for i in range(10)]
    rep = lambda v: np.ascontiguousarray(np.broadcast_to(f(v).reshape(1, -1), (128, f(v).size)))
    consts = np.zeros((128, NCONST), np.float32)
    ar = np.arange(128, dtype=np.float32)
    consts[:, OFF_IOTA:OFF_IOTA + 128] = ar[None, :]
    consts[:, OFF_IDENT:OFF_IDENT + 128] = np.eye(128, dtype=np.float32)
    consts[:, OFF_TRIB:OFF_TRIB + 128] = np.where(ar[None, :] > ar[:, None], -BIG, 0.0)
    consts[:, OFF_TRIM:OFF_TRIM + 128] = (ar[None, :] >= ar[:, None]).astype(np.float32)
    consts[:, OFF_INVQ:OFF_INVQ + 16] = (np.float32(500000.0) ** (-np.arange(0, 32, 2, dtype=np.float32) / np.float32(32)))[None, :]
    consts[:, OFF_INVI:OFF_INVI + 8] = (np.float32(500000.0) ** (-np.arange(0, 16, 2, dtype=np.float32) / np.float32(16)))[None, :]
    p2 = [2.0 ** (-k) for k in range(NIT)] + [2.0 ** (-(NIT - 1))]
    consts[:, OFF_POW2:OFF_POW2 + NIT + 1] = np.array(p2, np.float32)[None, :]

    shared = {
        "wada": _chunked(w_ada, 1024, 512),
        "badaf": np.ascontiguousarray(b_ada.reshape(48, 128).T),
        "badag": np.concatenate([rep(b_ada[2048:3072]), rep(b_ada[5120:6144])], axis=1),
        "gfm": np.concatenate([f(norm1_g)[0].reshape(8, 128).T, f(norm2_g)[0].reshape(8, 128).T], axis=1).copy(),
        "fng": rep(final_norm_g),
        "lng": np.concatenate([rep(f(gmlp_ln_g)[0]), rep(f(gmlp_ln_b)[0])], axis=1),
        "kln": np.concatenate([rep(f(idx_k_ln_g)[0]), rep(f(idx_k_ln_g)[0]), rep(f(idx_k_ln_b)[0]), rep(f(idx_k_ln_b)[0])], axis=1),
        "consts": consts,
        "wkv": _kc_layout(np.concatenate([k_, v_, ki_, ki_], axis=1), 1024),
        "wq": _kc_layout(np.concatenate([q_, qi_, wi_], axis=1), 1024),
        "wv": _kc_layout(av, 1024),
        "wuvg": _chunked(np.concatenate([au, ga_, gb_], axis=1), 1024, 128),
        "wpa": _chunked(f(w_proj_a)[0], 1024, 128),
        "wpb": _chunked(f(w_proj_b)[0], 1024, 128),
        "wout": _kc_layout(f(w_out)[0], 1024),
        "ws": np.ascontiguousarray(f(gmlp_w_s)[0].transpose(2, 0, 1).reshape(128, 1024)),
        "bs": np.ascontiguousarray(f(gmlp_b_s)[0].reshape(1, 1024)),
        "wffo": _kc_layout(f(w_ffn_out)[0], DFF),
    }
    wfi = f(w_ffn_in)[0]
    gcols = wfi[:, :DFF].reshape(1024, NFC, 128)
    ucols = wfi[:, DFF:].reshape(1024, NFC, 128)
    shared["wffi"] = _chunked(np.concatenate([gcols, ucols], axis=2).reshape(1024, NFC * 256), 1024, 256)

    in_maps = []
    owns = []
    for core in range(8):
        b, p = core // 2, core % 2
        own = _own_blocks(p)
        oth = _own_blocks(1 - p)
        perm = own + oth
        owns.append(own)
        xb = x[b].reshape(NB, 128, D)[perm]
        posb = positions[b].reshape(NB, 128)[perm]
        qrel = np.stack([(own[j] - oth[j]) * 128 + np.arange(128) for j in range(NOWN)], axis=1).astype(np.float32)
        m = dict(shared)
        m["x"] = np.ascontiguousarray(xb)
        m["pos"] = np.ascontiguousarray(posb.T.astype(np.int32))
        m["cT"] = np.ascontiguousarray(c[b].reshape(8, 128).T)
        m["qrel"] = np.ascontiguousarray(qrel)
        in_maps.append(m)
    return in_maps, owns


def kernel(**inputs):
    global _NC
    in_maps, owns = _prep(**inputs)
    if _NC is None:
        _NC = build()
    res = run_bass_kernel_spmd(_NC, in_maps, core_ids=list(range(8)))
    out = np.zeros((4, NB, 128, D), np.float32)
    for core in range(8):
        b = core // 2
        o_ = np.asarray(res.results[core]["out"]).reshape(NOWN, 128, D)
        for j, blk in enumerate(owns[core]):
            out[b, blk] = o_[j]
    return out.reshape(4, NB * 128, D)
```

```python
import numpy as np
from contextlib import ExitStack
import concourse.bass as bass
import concourse.mybir as mybir
from concourse.bass_utils import run_bass_kernel_spmd

F32 = mybir.dt.float32
BF16 = mybir.dt.bfloat16
I32 = mybir.dt.int32
AF = mybir.ActivationFunctionType
ALU = mybir.AluOpType

D = 1024
NB = 32
NOWN = 16
DFF = 2816
NFC = 22
EPS = 1e-6
NIT = 18
BIG = 1.0e30
TOPK = 256
MASKNEG = -30000.0
NDMA = 16
NDMA_SW = 8
CAST_DMA = True
TWO_PI = 2.0 * np.pi
C1 = 6.28125
C2 = float(TWO_PI - 6.28125)
PI_LO = 3.1415925
OFF_IOTA = 0
OFF_IDENT = 128
OFF_TRIB = 256
OFF_TRIM = 384
OFF_INVQ = 512
OFF_INVI = 528
OFF_POW2 = 536
NCONST = 536 + NIT + 1


class Sched:
    def __init__(self, nc, es):
        self.nc = nc
        self.eng = dict(pe=nc.tensor, act=nc.scalar, dve=nc.vector, pool=nc.gpsimd, sp=nc.sync)
        self.sem = {k: es.enter_context(nc.semaphore("s_" + k)) for k in self.eng}
        self.cnt = {k: 0 for k in self.eng}
        self.dsem = [es.enter_context(nc.semaphore("d%d" % i)) for i in range(NDMA)]
        self.dcnt = 0
        self.dsem_sw = [es.enter_context(nc.semaphore("w%d" % i)) for i in range(NDMA_SW)]
        self.dcnt_sw = 0
        self.waited = {}
        self.lastw = {}
        self.readers = {}
        self.selfsync = {"act", "dve", "pool"}
        self.done = False
        self.rec = None

    @staticmethod
    def _is_excl(k):
        return (isinstance(k, tuple) and k[0] == "bk") or k in ("tp0", "tp1")

    def _deps(self, reads, writes):
        toks = []
        for r in reads:
            if r in self.lastw:
                t = self.lastw[r]
                toks.append(t + (self._is_excl(r),))
        for w in writes:
            if w in self.lastw:
                t = self.lastw[w]
                toks.append(t + (self._is_excl(w),))
            toks.extend(t + (False,) for t in self.readers.get(w, {}).values())
        return toks

    def _wait(self, e, toks):
        best = {}
        for tk in toks:
            sid, sem, val, src = tk[0], tk[1], tk[2], tk[3]
            ex = tk[4] if len(tk) > 4 else False
            if src == e and (e not in self.selfsync or ex):
                continue
            if sid not in best or best[sid][1] < val:
                best[sid] = (sem, val)
        for sid, (sem, val) in best.items():
            if self.waited.get((e, sid), 0) >= val:
                continue
            self.eng[e].wait_ge(sem, val)
            self.waited[(e, sid)] = val

    def _record(self, tok, reads, writes):
        for w in writes:
            self.lastw[w] = tok
            self.readers[w] = {}
        for r in reads:
            if self._is_excl(r):
                self.lastw[r] = tok
                self.readers[r] = {}
                continue
            d = self.readers.setdefault(r, {})
            if tok[0] not in d or d[tok[0]][2] < tok[2]:
                d[tok[0]] = tok

    def begin_record(self):
        self.rec = [[]]

    def cut(self):
        if self.rec is not None and self.rec[-1]:
            self.rec.append([])

    def end_record(self):
        r = [g for g in self.rec if g]
        self.rec = None
        return r

    def play(self, group):
        for (kind, a, kw) in group:
            if kind == "op":
                self.op(*a, **kw)
            else:
                self.dma(*a, **kw)

    def op(self, e, name, reads=(), writes=(), **kw):
        if self.done:
            return
        if self.rec is not None:
            self.rec[-1].append(("op", (e, name, reads, writes), kw))
            return
        self._wait(e, self._deps(reads, writes))
        ins = getattr(self.eng[e], name)(**kw)
        self.cnt[e] += 1
        ins.then_inc(self.sem[e], 1)
        tok = (e, self.sem[e], self.cnt[e], e)
        self._record(tok, reads, writes)

    def dma(self, e, out, in_, reads=(), writes=()):
        if self.done:
            return
        if self.rec is not None:
            self.rec[-1].append(("dma", (e, out, in_, reads, writes), {}))
            return
        if e == "pool":
            i = self.dcnt_sw
            self.dcnt_sw += 1
            n, sems, tag = NDMA_SW, self.dsem_sw, "w"
        else:
            i = self.dcnt
            self.dcnt += 1
            n, sems, tag = NDMA, self.dsem, "d"
        slot = i % n
        val = 16 * (i // n + 1)
        toks = self._deps(reads, writes)
        if i >= n:
            toks.append(((tag, slot), sems[slot], val - 16, "dma"))
        self._wait(e, toks)
        self.eng[e].dma_start(out=out, in_=in_).then_inc(sems[slot], 16)
        tok = ((tag, slot), sems[slot], val, "dma")
        self._record(tok, reads, writes)

    def wait_all(self, e):
        self._wait(e, list(self.lastw.values()))

    def barrier(self):
        toks = [(k, self.sem[k], self.cnt[k], "bar") for k in self.eng if self.cnt[k] > 0]
        for slot in range(NDMA):
            n = (self.dcnt - slot + NDMA - 1) // NDMA
            if n > 0:
                toks.append((("d", slot), self.dsem[slot], 16 * n, "bar"))
        for slot in range(NDMA_SW):
            n = (self.dcnt_sw - slot + NDMA_SW - 1) // NDMA_SW
            if n > 0:
                toks.append((("w", slot), self.dsem_sw[slot], 16 * n, "bar"))
        for e in self.eng:
            self._wait(e, [t for t in toks if t[0] != e])


def build(stop=None):
    nc = bass.Bass("TRN2", target_bir_lowering=False)

    def din(name, shape, dt=F32):
        return nc.dram_tensor(name, list(shape), dt, kind="ExternalInput").ap()

    x_d = din("x", [NB, 128, D])
    pos_d = din("pos", [128, NB], I32)
    cT_d = din("cT", [128, 8])
    wada_d = din("wada", [12, 128, 4096])
    badaf_d = din("badaf", [128, 48])
    badag_d = din("badag", [128, 2048])
    gfm_d = din("gfm", [128, 16])
    fng_d = din("fng", [128, D])
    lng_d = din("lng", [128, 2048])
    kln_d = din("kln", [128, 256])
    consts_d = din("consts", [128, NCONST])
    qrel_d = din("qrel", [128, NOWN])
    wkv_d = din("wkv", [128, 8 * 640])
    wq_d = din("wq", [128, 8 * 1544])
    wv_d = din("wv", [128, 8 * 1024])
    wuvg_d = din("wuvg", [24, 128, 1024])
    wpa_d = din("wpa", [8, 128, 1024])
    wpb_d = din("wpb", [8, 128, 1024])
    wout_d = din("wout", [128, 8 * 1024])
    ws_d = din("ws", [128, 1024])
    bs_d = din("bs", [1, 1024])
    wffi_d = din("wffi", [NFC, 128, 2048])
    wffo_d = din("wffo", [128, NFC * 1024])
    out_d = nc.dram_tensor("out", [NOWN, 128, D], F32, kind="ExternalOutput").ap()
    dbg_d = nc.dram_tensor("dbg", [128, 32768], F32, kind="ExternalOutput").ap() if stop else None
    NWCH = 122
    wbf_d = nc.dram_tensor("wbf", [NWCH + 13, 128, 1024], BF16, kind="Internal").ap()
    wsrc = [wuvg_d[ch] for ch in range(24)]
    wsrc += [wv_d[:, i * 1024:(i + 1) * 1024] for i in range(8)]
    for cc in range(8):
        wsrc += [wpa_d[cc], wpb_d[cc]]
    wsrc += [wout_d[:, i * 1024:(i + 1) * 1024] for i in range(8)]
    for cp in range(NFC):
        wsrc += [wffi_d[cp][:, 0:1024], wffi_d[cp][:, 1024:2048]]
    wsrc += [wffo_d[:, i * 1024:(i + 1) * 1024] for i in range(NFC)]
    assert len(wsrc) == NWCH
    WB_U, WB_V, WB_A, WB_O, WB_FI, WB_FO, WB_Q = 0, 24, 32, 48, 56, 100, 122
    wcols = [1024] * NWCH
    for i in range(0, 8 * 1544, 1024):
        wsrc.append(wq_d[:, i:min(i + 1024, 8 * 1544)])
        wcols.append(min(1024, 8 * 1544 - i))
    NWALL = len(wsrc)

    with ExitStack() as es:
        S = Sched(nc, es)

        uniq = [0]

        def sb(stack, name, cols, dt=F32, parts=128):
            uniq[0] += 1
            return stack.enter_context(nc.sbuf_tensor("sb%d_%s" % (uniq[0], name), [parts, cols], dt))

        bk = [es.enter_context(nc.psum_tensor("bk%d" % i, [128, 512], F32)) for i in range(6)]
        tp = [es.enter_context(nc.psum_tensor("tp%d" % i, [128, 1024], BF16)) for i in range(2)]

        consts = sb(es, "consts", NCONST)
        ident = sb(es, "ident", 128, BF16)
        modF = sb(es, "modF", 48)
        GS = sb(es, "GS", 32)
        gfm = sb(es, "gfm", 16)
        sc2 = sb(es, "sc2", 16)
        xs = [sb(es, "xs%d" % i, D) for i in range(2)]
        xn = sb(es, "xn", D, BF16)
        junkb = sb(es, "junkb", D, BF16)
        st = sb(es, "st", 16)
        neghalf = sb(es, "neghalf", 1)
        screp = sb(es, "screp", 1024)
        ybT = sb(es, "ybT", 8 * 2048, BF16)
        stage = []
        stage_k = [0]

        def chk(name, aps):
            if stop != name or S.done:
                return
            S.barrier()
            o = 0
            for ap, n in aps:
                for i in range(0, n, 1024):
                    m = min(1024, n - i)
                    S.op("dve", "tensor_copy", out=xs[0][:, 0:m], in_=ap[:, i:i + m], reads=["dbgx"], writes=["dbgx"])
                    S.dma("sp", out=dbg_d[:, o:o + m], in_=xs[0][:, 0:m], reads=["dbgx"], writes=["dbgx"])
                    o += m
            S.barrier()
            S.done = True

        iota_row = consts[:, OFF_IOTA:OFF_IOTA + 128]
        trib = consts[:, OFF_TRIB:OFF_TRIB + 128]
        trim = consts[:, OFF_TRIM:OFF_TRIM + 128]

        cast_rr = [0]
        cast_engs = [["pool"]]

        def cast(dst_ap, dkey, src_ap, skey):
            e = cast_engs[0][cast_rr[0] % len(cast_engs[0])]
            cast_rr[0] += 1
            if e == "act":
                S.op("act", "activation", out=dst_ap, in_=src_ap, func=AF.Copy, reads=[skey], writes=[dkey])
            else:
                S.op(e, "tensor_copy", out=dst_ap, in_=src_ap, reads=[skey], writes=[dkey])

        def load_bf(dst, dkey, base, ncols, ks=None):
            for k in (range(ncols // 1024) if ks is None else ks):
                S.dma("sp", out=dst[:, k * 1024:(k + 1) * 1024], in_=wbf_d[base + k],
                      reads=[("wbf", base + k)], writes=[(dkey, k)])

        def wkeys(dkey, c0, c1):
            return [(dkey, k) for k in range(c0 // 1024, (c1 - 1) // 1024 + 1)]

        def load_cast(dst, dkey, src, ncols):
            for i in range(0, ncols, 1024):
                n = min(1024, ncols - i)
                k = stage_k[0] % len(stage)
                stage_k[0] += 1
                S.dma("sp", out=stage[k][:, 0:n], in_=src[:, i:i + n], writes=[("stage", k)])
                cast(dst[:, i:i + n], dkey, stage[k][:, 0:n], ("stage", k))

        S.dma("sp", out=consts[:], in_=consts_d[:, :], writes=["consts"])
        S.dma("sp", out=gfm[:], in_=gfm_d[:, :], writes=["gfm"])
        S.op("dve", "tensor_copy", out=ident[:], in_=consts[:, OFF_IDENT:OFF_IDENT + 128],
             reads=["consts"], writes=["ident"])
        S.op("pool", "memset", ap=neghalf[:], constant=-0.5, writes=["neghalf"])

        with ExitStack() as es0:
            cT = sb(es0, "cT", 8)
            sc = sb(es0, "sc", 8)
            ones_f = sb(es0, "ones_f", 128)
            wa = [sb(es0, "wa%d" % i, 4096) for i in range(2)]
            badaf = sb(es0, "badaf", 48)
            S.dma("sp", out=cT[:], in_=cT_d[:, :], writes=["cT"])
            S.dma("sp", out=badaf[:], in_=badaf_d[:, :], writes=["badaf"])
            S.op("act", "activation", out=sc[:], in_=cT[:], func=AF.Silu, reads=["cT"], writes=["sc"])
            S.op("pool", "memset", ap=ones_f[:], constant=1.0, writes=["ones_f"])
            for kc in range(8):
                S.op("dve", "tensor_copy", out=sc2[:, 2 * kc:2 * kc + 2],
                     in_=sc[:, kc:kc + 1].broadcast_to([128, 2]), reads=["sc"], writes=["sc2"])
                S.op("dve", "tensor_scalar", out=screp[:, kc * 128:(kc + 1) * 128], in0=ones_f[:],
                     scalar1=sc[:, kc:kc + 1], scalar2=None, op0=ALU.mult,
                     reads=["sc", "ones_f"], writes=["screp"])
            order = [0, 1, 2, 3, 6, 7, 8, 9]
            for n_, ch in enumerate(order):
                w = wa[n_ % 2]
                wk = ("wa", n_ % 2)
                S.dma("sp", out=w[:], in_=wada_d[ch], writes=[wk])
                if True:
                    ps = bk[2 + n_ % 2]
                    pk = ("bk", 2 + n_ % 2)
                    for jj in range(4):
                        for kc in range(8):
                            S.op("pe", "matmul", out=ps[:, 2 * jj:2 * jj + 2],
                                 lhsT=w[:, kc * 512 + jj * 128:kc * 512 + (jj + 1) * 128],
                                 rhs=sc2[:, 2 * kc:2 * kc + 2], start=(kc == 0), stop=(kc == 7),
                                 reads=["sc2", wk], writes=[pk])
                    psv = ps[:, 0:8].rearrange("p (a b) -> p a b", b=2)[:, :, 0]
                    S.op("dve", "tensor_tensor", out=modF[:, ch * 4:ch * 4 + 4], in0=psv,
                         in1=badaf[:, ch * 4:ch * 4 + 4], op=ALU.add, reads=[pk, "badaf"], writes=["modF"])
            for i_, (sh, scl) in enumerate([(0, 8), (24, 32)]):
                S.op("dve", "scalar_tensor_tensor", out=GS[:, 16 * i_:16 * i_ + 8], in0=modF[:, scl:scl + 8],
                     scalar=1.0, in1=gfm[:, 8 * i_:8 * i_ + 8], op0=ALU.add, op1=ALU.mult,
                     reads=["modF", "gfm"], writes=["GS"])
                S.op("dve", "tensor_copy", out=GS[:, 16 * i_ + 8:16 * i_ + 16], in_=modF[:, sh:sh + 8],
                     reads=["modF"], writes=["GS"])
            S.barrier()
        chk("setup", [(GS[:, :], 32), (modF[:, :], 48), (screp[:, :], 1024)])

        def rstd_from_sum(col_in, col_out, scale, key_in, key_out):
            S.op("dve", "tensor_scalar", out=st[:, col_out:col_out + 1], in0=st[:, col_in:col_in + 1],
                 scalar1=scale, scalar2=EPS, op0=ALU.mult, op1=ALU.add, reads=[key_in], writes=[key_out + "_v"])
            S.op("act", "activation", out=st[:, col_out:col_out + 1], in_=st[:, col_out:col_out + 1], func=AF.Sqrt,
                 reads=[key_out + "_v"], writes=[key_out + "_s"])
            S.op("dve", "reciprocal", out=st[:, col_out:col_out + 1], in_=st[:, col_out:col_out + 1],
                 reads=[key_out + "_s"], writes=[key_out])

        def norm_block(xap, xkey, gofs, hT, hkey, hstride, hofs):
            S.op("act", "activation", out=junkb[:], in_=xap, func=AF.Square, accum_out=st[:, 0:1],
                 reads=[xkey], writes=["junkb", "ss"])
            rstd_from_sum(0, 1, 1.0 / D, "ss", "rstd")
            S.op("act", "activation", out=xn[:], in_=xap, func=AF.Copy, scale=st[:, 1:2],
                 reads=[xkey, "rstd"], writes=["xn"])
            for kc in range(8):
                S.op("pe", "transpose", out=tp[0][:, kc * 128:(kc + 1) * 128], in_=xn[:, kc * 128:(kc + 1) * 128],
                     identity=ident[:], reads=["xn", "ident"], writes=["tp0"])
            for kc in range(8):
                S.op("dve", "tensor_scalar", reads=["tp0", "GS"], writes=[(hkey, kc)],
                     out=hT[:, kc * hstride + hofs:kc * hstride + hofs + 128],
                     in0=tp[0][:, kc * 128:(kc + 1) * 128],
                     scalar1=GS[:, gofs + kc:gofs + kc + 1], scalar2=GS[:, gofs + 8 + kc:gofs + 9 + kc],
                     op0=ALU.mult, op1=ALU.add)

        def rope(src, skey, dst, dkey, nh, hd, rd, cs, sn, tmp):
            S.op("act", "activation", out=dst, in_=src, func=AF.Copy, reads=[skey], writes=[dkey])
            h2 = rd // 2
            s3 = src.rearrange("p (h d) -> p h d", d=hd)
            d3 = dst.rearrange("p (h d) -> p h d", d=hd)
            x1, x2 = s3[:, :, 0:h2], s3[:, :, h2:rd]
            cb = cs.unsqueeze(1).broadcast_to([128, nh, h2])
            sbb = sn.unsqueeze(1).broadcast_to([128, nh, h2])
            t = [tmp[:, i * nh * h2:(i + 1) * nh * h2].rearrange("p (h d) -> p h d", d=h2) for i in range(4)]
            S.op("dve", "tensor_tensor", out=t[0], in0=x1, in1=cb, op=ALU.mult, reads=[skey, "ropetab"], writes=["rt0"])
            S.op("dve", "tensor_tensor", out=t[1], in0=x2, in1=sbb, op=ALU.mult, reads=[skey, "ropetab"], writes=["rt1"])
            S.op("dve", "tensor_tensor", out=t[2], in0=x2, in1=cb, op=ALU.mult, reads=[skey, "ropetab"], writes=["rt2"])
            S.op("dve", "tensor_tensor", out=t[3], in0=x1, in1=sbb, op=ALU.mult, reads=[skey, "ropetab"], writes=["rt3"])
            S.op("dve", "tensor_tensor", out=d3[:, :, 0:h2], in0=t[0], in1=t[1], op=ALU.subtract,
                 reads=["rt0", "rt1", dkey], writes=[dkey])
            S.op("dve", "tensor_tensor", out=d3[:, :, h2:rd], in0=t[2], in1=t[3], op=ALU.add,
                 reads=["rt2", "rt3", dkey], writes=[dkey])

        def record(fn, *a):
            S.begin_record()
            fn(*a)
            return S.end_record()

        def interleave(lists, weights=None):
            if weights is None:
                weights = [[1.0] * len(l) for l in lists]
            idx = [0] * len(lists)
            cum = [0.0] * len(lists)
            tot = [max(1e-9, sum(w)) for w in weights]
            while True:
                cand = [i for i in range(len(lists)) if idx[i] < len(lists[i])]
                if not cand:
                    break
                i = min(cand, key=lambda i_: cum[i_] / tot[i_])
                S.play(lists[i][idx[i]])
                cum[i] += weights[i][idx[i]]
                idx[i] += 1


        with ExitStack() as es1:
            KT = sb(es1, "KT", 2 * 4096, BF16)
            VA = sb(es1, "VA", NB * 2 * 132, BF16)
            kiT = sb(es1, "kiT", 4096, BF16)
            cosq = sb(es1, "cosq", NB * 16)
            sinq = sb(es1, "sinq", NB * 16)
            cosi = sb(es1, "cosi", NB * 8)
            sini = sb(es1, "sini", NB * 8)
            rtmp = sb(es1, "rtmp", 4 * 8 * 16)
            hTb = sb(es1, "hTb", 1024, BF16)
            kln = sb(es1, "kln", 256)
            S.dma("sp", out=kln[:], in_=kln_d[:, :], writes=["kln"])
            S.op("pool", "memset", ap=VA[:], constant=1.0, writes=["VA"])

            with ExitStack() as est:
                posi = sb(est, "posi", NB, I32)
                posf = sb(est, "posf", NB)
                ang = sb(est, "ang", NB * 16)
                nf = sb(est, "nf", NB * 16)
                ni = sb(est, "ni", NB * 16, I32)
                S.dma("sp", out=posi[:], in_=pos_d[:, :], writes=["posi"])
                S.op("dve", "tensor_copy", out=posf[:], in_=posi[:], reads=["posi"], writes=["posf"])
                for (nfreq, off, ctab, stab) in [(16, OFF_INVQ, cosq, sinq), (8, OFF_INVI, cosi, sini)]:
                    n_ = NB * nfreq
                    a3 = ang[:, 0:n_].rearrange("p (b i) -> p b i", i=nfreq)
                    S.op("dve", "tensor_tensor", out=a3, in0=posf[:].unsqueeze(2).broadcast_to([128, NB, nfreq]),
                         in1=consts[:, off:off + nfreq].unsqueeze(1).broadcast_to([128, NB, nfreq]), op=ALU.mult,
                         reads=["posf", "consts", "ang"], writes=["ang"])
                    for (shift, tab) in [(0.0, stab), (np.pi / 2, ctab)]:
                        S.op("dve", "tensor_scalar", out=nf[:, 0:n_], in0=ang[:, 0:n_], scalar1=shift,
                             scalar2=1.0 / TWO_PI, op0=ALU.add, op1=ALU.mult, reads=["ang", "nf"], writes=["nf"])
                        S.op("dve", "tensor_copy", out=ni[:, 0:n_], in_=nf[:, 0:n_], reads=["nf", "ni"], writes=["ni"])
                        S.op("dve", "tensor_copy", out=nf[:, 0:n_], in_=ni[:, 0:n_], reads=["ni"], writes=["nf"])
                        S.op("dve", "scalar_tensor_tensor", out=tab[:], in0=nf[:, 0:n_], scalar=-C1, in1=ang[:, 0:n_],
                             op0=ALU.mult, op1=ALU.add, reads=["nf", "ang"], writes=["ropetab"])
                        S.op("dve", "scalar_tensor_tensor", out=tab[:], in0=nf[:, 0:n_], scalar=-C2, in1=tab[:],
                             op0=ALU.mult, op1=ALU.add, reads=["nf", "ropetab"], writes=["ropetab"])
                        S.op("dve", "tensor_scalar", out=tab[:], in0=tab[:], scalar1=shift, scalar2=PI_LO,
                             op0=ALU.add, op1=ALU.min, reads=["ropetab"], writes=["ropetab"])
                        S.op("dve", "tensor_scalar", out=tab[:], in0=tab[:], scalar1=-PI_LO, scalar2=None,
                             op0=ALU.max, reads=["ropetab"], writes=["ropetab"])
                        S.op("act", "activation", out=tab[:], in_=tab[:], func=AF.Sin, reads=["ropetab"], writes=["ropetab"])
                S.barrier()

            chk("tables", [(cosq[:, :], 512), (sinq[:, :], 512), (cosi[:, :], 256), (sini[:, :], 256)])
            with ExitStack() as esp1:
                stage[:] = [sb(esp1, "stage%d" % i, 1024) for i in range(2)]
                wkv = sb(esp1, "wkv", 8 * 640, BF16)
                k_r = sb(esp1, "k_r", 256, BF16)
                kif = sb(esp1, "kif", 128)
                ki_r = sb(esp1, "ki_r", 128, BF16)
                bst = sb(esp1, "bst", 8)
                load_cast(wkv, "wkv", wkv_d, 8 * 640)
                stage2 = [sb(esp1, "cstage%d" % i, 1024) for i in range(2)]
                cbuf = [sb(esp1, "cbuf%d" % i, 1024, BF16) for i in range(2)]

                def conv_steps(i0, i1):
                    for i in range(i0, i1):
                        n = wcols[i]
                        if CAST_DMA:
                            S.dma("pool", out=wbf_d[i][:, 0:n], in_=wsrc[i], writes=[("wbf", i)])
                            continue
                        if i == 0:
                            S.dma("sp", out=stage2[0][:, 0:wcols[0]], in_=wsrc[0], writes=[("cstage", 0)])
                        if i + 1 < NWALL:
                            S.dma("sp", out=stage2[(i + 1) % 2][:, 0:wcols[i + 1]], in_=wsrc[i + 1],
                                  writes=[("cstage", (i + 1) % 2)])
                        if i % 3 == 2:
                            S.op("act", "activation", out=cbuf[i % 2][:, 0:n], in_=stage2[i % 2][:, 0:n], func=AF.Copy,
                                 reads=[("cstage", i % 2)], writes=[("cbuf", i % 2)])
                        else:
                            S.op("pool", "tensor_copy", out=cbuf[i % 2][:, 0:n], in_=stage2[i % 2][:, 0:n],
                                 reads=[("cstage", i % 2)], writes=[("cbuf", i % 2)])
                        S.dma("sp", out=wbf_d[i][:, 0:n], in_=cbuf[i % 2][:, 0:n], reads=[("cbuf", i % 2)],
                              writes=[("wbf", i)])

                hTb2 = [hTb, sb(esp1, "hTb_b", 1024, BF16)]
                S.dma("sp", out=xs[0][:], in_=x_d[0], writes=[("xs", 0)])

                def p1a(blk):
                    xb = xs[blk % 2]
                    xk = ("xs", blk % 2)
                    hT_, hk_ = hTb2[blk % 2], "hTb%d" % (blk % 2)
                    if blk + 1 < NB:
                        S.dma("sp", out=xs[(blk + 1) % 2][:], in_=x_d[blk + 1], writes=[("xs", (blk + 1) % 2)])
                    if blk < 13:
                        conv_steps(WB_Q + blk, WB_Q + blk + 1)
                    norm_block(xb[:], xk, 0, hT_, hk_, 128, 0)
                    S.cut()
                    pa, pb = bk[2 * (blk % 2)], bk[2 * (blk % 2) + 1]
                    ka, kb_ = ("bk", 2 * (blk % 2)), ("bk", 2 * (blk % 2) + 1)
                    for kc in range(8):
                        S.op("pe", "matmul", out=pa[:, 0:512], lhsT=hT_[:, kc * 128:(kc + 1) * 128],
                             rhs=wkv[:, kc * 640:kc * 640 + 512], start=(kc == 0), stop=(kc == 7),
                             reads=[(hk_, kc), "wkv"], writes=[ka])
                    for kc in range(8):
                        S.op("pe", "matmul", out=pb[:, 0:128], lhsT=hT_[:, kc * 128:(kc + 1) * 128],
                             rhs=wkv[:, kc * 640 + 512:kc * 640 + 640], start=(kc == 0), stop=(kc == 7),
                             reads=[(hk_, kc), "wkv"], writes=[kb_])
                    S.cut()

                def p1b(blk):
                    pa, pb = bk[2 * (blk % 2)], bk[2 * (blk % 2) + 1]
                    ka, kb_ = ("bk", 2 * (blk % 2)), ("bk", 2 * (blk % 2) + 1)
                    rope(pa[:, 0:256], ka, k_r[:], "k_r", 2, 128, 32, cosq[:, blk * 16:(blk + 1) * 16],
                         sinq[:, blk * 16:(blk + 1) * 16], rtmp)
                    S.cut()
                    for g in range(2):
                        S.op("pe", "transpose", out=tp[1][:, g * 128:(g + 1) * 128], in_=k_r[:, g * 128:(g + 1) * 128],
                             identity=ident[:], reads=["k_r", "ident"], writes=["tp1"])
                    S.op("dve", "tensor_copy", out=KT[:].rearrange("p (g s) -> p g s", g=2)[:, :, blk * 128:(blk + 1) * 128],
                         in_=tp[1][:, 0:256].rearrange("p (g s) -> p g s", g=2),
                         reads=["tp1"], writes=["KT"])
                    S.cut()
                    S.op("act", "activation",
                         out=VA[:, blk * 264:(blk + 1) * 264].rearrange("p (g d) -> p g d", g=2)[:, :, 0:128],
                         in_=pa[:, 256:512].rearrange("p (g d) -> p g d", g=2), func=AF.Copy,
                         reads=[ka], writes=["VA"])
                    S.op("dve", "bn_stats", out=bst[:, 0:6], in_=pb[:, 0:64], reads=[kb_], writes=["bst"])
                    S.op("dve", "bn_aggr", out=st[:, 4:6], in_=bst[:, 0:6], reads=["bst"], writes=["kmv"])
                    S.op("dve", "tensor_scalar", out=st[:, 6:7], in0=st[:, 5:6], scalar1=EPS, scalar2=None,
                         op0=ALU.add, reads=["kmv"], writes=["kv_v"])
                    S.op("act", "activation", out=st[:, 6:7], in_=st[:, 6:7], func=AF.Sqrt,
                         reads=["kv_v"], writes=["kv_s"])
                    S.op("dve", "reciprocal", out=st[:, 6:7], in_=st[:, 6:7], reads=["kv_s"], writes=["krstd"])
                    S.cut()
                    S.op("dve", "tensor_scalar", out=kif[:], in0=pb[:, 0:128], scalar1=st[:, 4:5], scalar2=st[:, 6:7],
                         op0=ALU.subtract, op1=ALU.mult, reads=[kb_, "kmv", "krstd"], writes=["kif"])
                    S.op("dve", "tensor_tensor", out=kif[:], in0=kif[:], in1=kln[:, 0:128], op=ALU.mult,
                         reads=["kif", "kln"], writes=["kif"])
                    S.op("dve", "tensor_tensor", out=kif[:], in0=kif[:], in1=kln[:, 128:256], op=ALU.add,
                         reads=["kif", "kln"], writes=["kif"])
                    S.cut()
                    rope(kif[:], "kif", ki_r[:], "ki_r", 2, 64, 16, cosi[:, blk * 8:(blk + 1) * 8],
                         sini[:, blk * 8:(blk + 1) * 8], rtmp)
                    S.cut()
                    S.op("pe", "transpose", out=tp[1][:, 256:384], in_=ki_r[:], identity=ident[:],
                         reads=["ki_r", "ident"], writes=["tp1"])
                    S.op("dve", "tensor_copy", out=kiT[:, blk * 128:(blk + 1) * 128], in_=tp[1][:, 256:384],
                         reads=["tp1"], writes=["kiT"])
                    S.cut()

                for r in range(NB + 1):
                    ls = []
                    if 0 <= r - 1 < NB:
                        ls.append(record(p1b, r - 1))
                    if r < NB:
                        ls.append(record(p1a, r))
                    interleave(ls)
                S.barrier()

            chk("p1", [(KT[:, :], 8192), (kiT[:, :], 4096), (VA[:, :], NB * 264), (cosq[:, :], 512), (sinq[:, :], 512), (cosi[:, :], 256), (sini[:, :], 256)])
            with ExitStack() as esp2:
                wq = sb(esp2, "wq", 8 * 1544, BF16)
                score2 = [sb(esp2, "score%d" % i, 4096) for i in range(2)]
                mask = sb(esp2, "mask", 4096, BF16)
                maskT = [sb(esp2, "maskT%d" % i, 4096, BF16) for i in range(2)]
                rbuf = [sb(esp2, "rbuf%d" % i, 512) for i in range(2)]
                ebuf = [sb(esp2, "ebuf%d" % i, 512, BF16) for i in range(2)]
                ptb = [sb(esp2, "ptb%d" % i, 512, BF16) for i in range(2)]
                q_r = sb(esp2, "q_r", 1024, BF16)
                qi_r = sb(esp2, "qi_r", 512, BF16)
                qT = [sb(esp2, "qT%d" % i, 1024, BF16) for i in range(3)]
                qiT = sb(esp2, "qiT", 1024, BF16)
                S.op("pool", "memset", ap=qiT[:], constant=0.0, writes=["qiT"])
                y_b = sb(esp2, "y_b", 1024, BF16)
                wab = sb(esp2, "wab", 8)
                wsg = sb(esp2, "wsg", 8)
                qrel = sb(esp2, "qrel", NOWN)
                halfs2 = [sb(esp2, "halfs%d" % i, NIT + 1) for i in range(2)]
                bisA2 = [sb(esp2, "bisA%d" % i, 2) for i in range(2)]
                bis = sb(esp2, "bis", 8)
                obias = sb(esp2, "obias", 128)
                rc = sb(esp2, "rc", 4)
                S.dma("sp", out=qrel[:], in_=qrel_d[:, :], writes=["qrel"])
                for k_ in range(13):
                    n_ = wcols[WB_Q + k_]
                    S.dma("sp", out=wq[:, k_ * 1024:k_ * 1024 + n_], in_=wbf_d[WB_Q + k_][:, 0:n_],
                          reads=[("wbf", WB_Q + k_)], writes=[("wq", k_)])
                mk3 = mask[:].rearrange("p (a s) -> p a s", a=2)
                c0 = float(64 ** -0.5 * 8 ** -0.5)
                att_scale = float(128 ** -0.5)

                def stage_A(j):
                    L = 128 * (j + 1)
                    xb = xs[j % 2]
                    xk = ("xs", j % 2)
                    qTj, qTk = qT[j % 3], ("qT", j % 3)
                    score, SK = score2[j % 2], ("scr", j % 2)
                    sc3 = score[:].rearrange("p (a s) -> p a s", a=2)
                    halfs, HK = halfs2[j % 2], ("halfs", j % 2)
                    bisA = bisA2[j % 2]
                    S.dma("sp", out=xb[:], in_=x_d[j], writes=[xk])
                    norm_block(xb[:], xk, 0, hTb, "hTb", 128, 0)
                    S.cut()
                    for n_ in range(2):
                        for kc in range(8):
                            S.op("pe", "matmul", out=bk[n_][:, 0:512], lhsT=hTb[:, kc * 128:(kc + 1) * 128],
                                 rhs=wq[:, kc * 1544 + n_ * 512:kc * 1544 + (n_ + 1) * 512], start=(kc == 0), stop=(kc == 7),
                                 reads=[("hTb", kc)] + wkeys("wq", kc * 1544 + n_ * 512, kc * 1544 + (n_ + 1) * 512),
                                 writes=[("bk", n_)])
                    S.cut()
                    cq, sq_ = cosq[:, j * 16:(j + 1) * 16], sinq[:, j * 16:(j + 1) * 16]
                    rope(bk[0][:, 0:512], ("bk", 0), q_r[:, 0:512], "q_r0", 4, 128, 32, cq, sq_, rtmp)
                    S.cut()
                    rope(bk[1][:, 0:512], ("bk", 1), q_r[:, 512:1024], "q_r1", 4, 128, 32, cq, sq_, rtmp)
                    S.cut()
                    for kc in range(8):
                        S.op("pe", "matmul", out=bk[0][:, 0:512], lhsT=hTb[:, kc * 128:(kc + 1) * 128],
                             rhs=wq[:, kc * 1544 + 1024:kc * 1544 + 1536], start=(kc == 0), stop=(kc == 7),
                             reads=[("hTb", kc)] + wkeys("wq", kc * 1544 + 1024, kc * 1544 + 1536), writes=[("bk", 0)])
                    for kc in range(8):
                        S.op("pe", "matmul", out=bk[1][:, 0:8], lhsT=hTb[:, kc * 128:(kc + 1) * 128],
                             rhs=wq[:, kc * 1544 + 1536:kc * 1544 + 1544], start=(kc == 0), stop=(kc == 7),
                             reads=[("hTb", kc)] + wkeys("wq", kc * 1544 + 1536, kc * 1544 + 1544), writes=[("bk", 1)])
                    S.cut()
                    rope(bk[0][:, 0:512], ("bk", 0), qi_r[:], "qi_r", 8, 64, 16, cosi[:, j * 8:(j + 1) * 8],
                         sini[:, j * 8:(j + 1) * 8], rtmp)
                    S.op("act", "activation", out=wab[:], in_=bk[1][:, 0:8], func=AF.Abs, scale=c0,
                         reads=[("bk", 1)], writes=["wab"])
                    S.op("act", "activation", out=wsg[:], in_=bk[1][:, 0:8], func=AF.Sign,
                         reads=[("bk", 1)], writes=["wsg"])
                    S.cut()
                    for h in range(8):
                        S.op("pe", "transpose", out=tp[0][:, h * 128:(h + 1) * 128], in_=q_r[:, h * 128:(h + 1) * 128],
                             identity=ident[:], reads=["q_r0", "q_r1", "ident"], writes=["tp0"])
                    S.op("dve", "tensor_copy", out=qTj[:], in_=tp[0][:], reads=["tp0"], writes=[qTk])
                    for h2 in range(4):
                        S.op("pe", "transpose", out=tp[0][:, h2 * 128:(h2 + 1) * 128], in_=qi_r[:, h2 * 128:(h2 + 1) * 128],
                             identity=ident[:], reads=["qi_r", "ident"], writes=["tp0"])
                    qz = qiT[:].rearrange("p (h t) -> p h t", h=8)
                    t3 = tp[0][:, 0:512].rearrange("p (h t) -> p h t", h=4)
                    S.op("dve", "tensor_copy", out=qz[0:64, 0:8:2, :], in_=t3[0:64, :, :], reads=["tp0"], writes=["qiT"])
                    S.op("dve", "tensor_copy", out=qz[64:128, 1:8:2, :], in_=t3[64:128, :, :], reads=["tp0"], writes=["qiT"])
                    S.cut()
                    chunks = []
                    for a in range(2):
                        for c_ in range(0, L, 512):
                            chunks.append((a, c_, min(512, L - c_)))
                    ci = 0
                    for (a, c_, n_) in chunks:
                        ko = a * 2048 + c_
                        for h in range(8):
                            pl = bk[ci % 2]
                            plk = ("bk", ci % 2)
                            rb = rbuf[ci % 2]
                            rk = ("rbuf", ci % 2)
                            ci += 1
                            r0 = (h % 2) * 64
                            S.op("pe", "matmul", out=pl[:, 0:n_], lhsT=qiT[:, h * 128:(h + 1) * 128],
                                 rhs=kiT[:, ko:ko + n_], start=True, stop=True,
                                 reads=["qiT", "kiT"], writes=[plk])
                            S.op("act", "activation", out=rb[:, 0:n_], in_=pl[:, 0:n_], func=AF.Relu,
                                 scale=wab[:, h:h + 1], reads=[plk, "wab"], writes=[rk])
                            if h == 0:
                                S.op("dve", "tensor_scalar", out=score[:, ko:ko + n_], in0=rb[:, 0:n_],
                                     scalar1=wsg[:, 0:1], scalar2=None, op0=ALU.mult,
                                     reads=[rk, "wsg"], writes=[SK])
                            else:
                                S.op("dve", "scalar_tensor_tensor", out=score[:, ko:ko + n_], in0=rb[:, 0:n_],
                                     scalar=wsg[:, h:h + 1], in1=score[:, ko:ko + n_], op0=ALU.mult, op1=ALU.add,
                                     reads=[rk, "wsg", SK], writes=[SK])
                            S.cut()
                    S.op("dve", "tensor_reduce", out=bisA[:, 0:1], in_=sc3[:, :, 0:L], axis=mybir.AxisListType.XY,
                         op=ALU.max, apply_absolute_value=True, reads=[SK], writes=[("amax", j % 2)])
                    S.op("dve", "tensor_scalar", out=bisA[:, 1:2], in0=bisA[:, 0:1], scalar1=1.001, scalar2=1e-6,
                         op0=ALU.mult, op1=ALU.add, reads=[("amax", j % 2)], writes=[("A", j % 2)])
                    S.op("dve", "tensor_scalar", out=halfs[:], in0=consts[:, OFF_POW2:OFF_POW2 + NIT + 1],
                         scalar1=bisA[:, 1:2], scalar2=None, op0=ALU.mult, reads=[("A", j % 2), "consts"], writes=[HK])
                    S.op("dve", "tensor_tensor", out=score[:, j * 128:(j + 1) * 128], in0=score[:, j * 128:(j + 1) * 128],
                         in1=trib, op=ALU.add, reads=[SK, "consts"], writes=[SK])
                    S.op("dve", "tensor_scalar", out=obias[:], in0=iota_row, scalar1=qrel[:, j:j + 1], scalar2=-BIG,
                         op0=ALU.is_gt, op1=ALU.mult, reads=["consts", "qrel"], writes=["obias"])
                    S.op("dve", "tensor_tensor", out=score[:, 2048 + j * 128:2048 + (j + 1) * 128],
                         in0=score[:, 2048 + j * 128:2048 + (j + 1) * 128], in1=obias[:], op=ALU.add,
                         reads=[SK, "obias"], writes=[SK])
                    S.cut()

                def stage_B(j):
                    L = 128 * (j + 1)
                    mTj, mTk = maskT[j % 2], ("maskT", j % 2)
                    score, SK = score2[j % 2], ("scr", j % 2)
                    sc3 = score[:].rearrange("p (a s) -> p a s", a=2)
                    halfs, HK = halfs2[j % 2], ("halfs", j % 2)
                    S.op("dve", "memset", ap=bis[:, 2:3], constant=0.0, writes=["mid"])
                    for k in range(NIT):
                        S.op("dve", "tensor_scalar", out=mask[:, 0:L], in0=score[:, 0:L], scalar1=bis[:, 2:3],
                             scalar2=None, op0=ALU.is_ge, op1=ALU.add, accum_out=bis[:, 3:4],
                             reads=[SK, "mid"], writes=["maskA", "cnt"])
                        S.op("act", "activation", out=mask[:, 2048:2048 + L], in_=score[:, 2048:2048 + L], func=AF.Sign,
                             scale=-1.0, bias=bis[:, 2:3], accum_out=bis[:, 5:6],
                             reads=[SK, "mid"], writes=["maskB", "negs"])
                        S.op("dve", "scalar_tensor_tensor", out=bis[:, 6:7], in0=bis[:, 5:6], scalar=-0.5,
                             in1=bis[:, 3:4], op0=ALU.mult, op1=ALU.add, reads=["negs", "cnt"], writes=["cnt2"])
                        S.op("dve", "scalar_tensor_tensor", out=bis[:, 4:5], in0=bis[:, 6:7], scalar=TOPK - 0.5 - L / 2.0,
                             in1=halfs[:, k:k + 1], op0=ALU.is_ge, op1=ALU.mult, reads=["cnt2", HK], writes=["btmp"])
                        S.op("dve", "scalar_tensor_tensor", out=bis[:, 2:3], in0=bis[:, 4:5], scalar=halfs[:, k + 1:k + 2],
                             in1=bis[:, 2:3], op0=ALU.subtract, op1=ALU.add, reads=["btmp", HK, "mid"], writes=["mid"])
                        S.cut()
                    S.op("dve", "tensor_scalar", out=mk3[:, :, 0:L], in0=sc3[:, :, 0:L], scalar1=bis[:, 2:3],
                         scalar2=MASKNEG, op0=ALU.is_lt, op1=ALU.mult, reads=[SK, "mid"], writes=["maskA", "maskB"])
                    S.cut()
                    kbs = list(range(j + 1)) + list(range(16, 16 + j + 1))
                    for i0 in range(0, len(kbs), 8):
                        grp = kbs[i0:i0 + 8]
                        for ii, kb in enumerate(grp):
                            S.op("pe", "transpose", out=tp[0][:, ii * 128:(ii + 1) * 128], in_=mask[:, kb * 128:(kb + 1) * 128],
                                 identity=ident[:], reads=["maskA", "maskB", "ident"], writes=["tp0"])
                        runs = []
                        for ii, kb in enumerate(grp):
                            if runs and runs[-1][1] + runs[-1][2] == kb:
                                runs[-1][2] += 1
                            else:
                                runs.append([ii, kb, 1])
                        for (ii, kb, cnt_) in runs:
                            S.op("dve", "tensor_copy", reads=["tp0"], writes=[mTk],
                                 out=mTj[:, kb * 128:(kb + cnt_) * 128], in_=tp[0][:, ii * 128:(ii + cnt_) * 128])
                        S.cut()

                def stage_C(j):
                    qTj, qTk = qT[j % 3], ("qT", j % 3)
                    mTj, mTk = maskT[j % 2], ("maskT", j % 2)
                    kbs = list(range(j + 1)) + list(range(16, 16 + j + 1))
                    ai = 0
                    for g in range(2):
                        ob = [bk[2], bk[3]]
                        for idx, kb in enumerate(kbs):
                            ps_ = bk[4 + ai % 2]
                            pk = ("bk", 4 + ai % 2)
                            eb, ek = ebuf[ai % 2], ("ebuf", ai % 2)
                            pt_, ptk = ptb[ai % 2], ("ptb", ai % 2)
                            ai += 1
                            S.op("pe", "matmul", out=ps_[:, 0:512], lhsT=KT[:, g * 4096 + kb * 128:g * 4096 + (kb + 1) * 128],
                                 rhs=qTj[:, g * 512:(g + 1) * 512], start=True, stop=False,
                                 reads=["KT", qTk], writes=[pk])
                            for h in range(4):
                                S.op("pe", "matmul", out=ps_[:, h * 128:(h + 1) * 128], lhsT=ident[:],
                                     rhs=mTj[:, kb * 128:(kb + 1) * 128], start=False, stop=(h == 3),
                                     reads=["ident", mTk], writes=[pk])
                            S.op("act", "activation", out=pt_[:], in_=ps_[:, 0:512], func=AF.Exp, scale=att_scale,
                                 reads=[pk], writes=[ptk])
                            for h in range(4):
                                S.op("pe", "matmul", out=ob[h // 2][:, (h % 2) * 256:(h % 2) * 256 + 129],
                                     lhsT=pt_[:, h * 128:(h + 1) * 128],
                                     rhs=VA[:, (kb * 2 + g) * 132:(kb * 2 + g) * 132 + 129],
                                     start=(idx == 0 and h % 2 == 0), stop=(idx == len(kbs) - 1),
                                     skip_group_check=True, reads=[ptk, "VA"], writes=[("bk", 2 + h // 2)])
                            S.cut()
                        for hh in range(2):
                            S.op("dve", "reciprocal", out=rc[:, 2 * hh:2 * hh + 2],
                                 in_=ob[hh][:, 0:512].rearrange("p (a d) -> p a d", a=2)[:, :, 128],
                                 reads=[("bk", 2 + hh)], writes=["rc"])
                        for h in range(4):
                            S.op("act", "activation", out=y_b[:, (g * 4 + h) * 128:(g * 4 + h + 1) * 128],
                                 in_=ob[h // 2][:, (h % 2) * 256:(h % 2) * 256 + 128], func=AF.Copy,
                                 scale=rc[:, h:h + 1], reads=[("bk", 2 + h // 2), "rc"], writes=["y_b"])
                        S.cut()
                    for h in range(8):
                        S.op("pe", "transpose", out=tp[1][:, h * 128:(h + 1) * 128], in_=y_b[:, h * 128:(h + 1) * 128],
                             identity=ident[:], reads=["y_b", "ident"], writes=["tp1"])
                    S.op("dve", "tensor_copy", out=ybT[:].rearrange("p (c t) -> p c t", c=8)[:, :, j * 128:(j + 1) * 128],
                         in_=tp[1][:].rearrange("p (c t) -> p c t", c=8), reads=["tp1"], writes=["ybT"])
                    S.cut()

                def conv2(i0, i1):
                    for i in range(i0, i1):
                        S.dma("pool", out=wbf_d[i], in_=wsrc[i], writes=[("wbf", i)])
                        S.cut()

                for r in range(NOWN + 2):
                    ls = []
                    if r < NOWN:
                        ls.append(record(conv2, (NWCH * r) // NOWN, (NWCH * (r + 1)) // NOWN))
                    if 0 <= r - 2 < NOWN:
                        ls.append(record(stage_C, r - 2))
                    if 0 <= r - 1 < NOWN:
                        ls.append(record(stage_B, r - 1))
                    if r < NOWN:
                        ls.append(record(stage_A, r))
                    interleave(ls)
                S.barrier()

        chk("p2a", [(ybT[:, :], 16384)])
        with ExitStack() as es3:
            cast_engs[0] = ["dve", "act"]
            stage[:] = []
            NWST, NWFI = 6, 4
            xq = sb(es3, "xq", 4 * D)
            tmpa = sb(es3, "tmpa", 512)
            tmpb = sb(es3, "tmpb", 512)
            onesz = sb(es3, "onesz", 128, BF16)
            bs2 = sb(es3, "bs2", 1024, BF16)
            S.op("pool", "memset", ap=onesz[:], constant=0.0, writes=["onesz"])
            S.op("pool", "memset", ap=onesz[0:1, :], constant=1.0, writes=["onesz"])
            S.op("pool", "memset", ap=onesz[32:33, :], constant=1.0, writes=["onesz"])
            with ExitStack() as esbs:
                bsf = sb(esbs, "bsf", 1024)
                bhf = sb(esbs, "bhf", 1024)
                blo = sb(esbs, "blo", 1024, BF16)
                S.op("pool", "memset", ap=bsf[:], constant=0.0, writes=["bsf"])
                S.dma("sp", out=bsf[0:1, :], in_=bs_d[:, :], reads=["bsf"], writes=["bsf0"])
                S.dma("sp", out=bsf[32:33, :], in_=bs_d[:, :], reads=["bsf"], writes=["bsf32"])
                S.op("dve", "tensor_copy", out=bs2[:], in_=bsf[:], reads=["bsf", "bsf0", "bsf32"], writes=["bs2"])
                S.op("dve", "tensor_copy", out=bhf[:], in_=bs2[:], reads=["bs2"], writes=["bhf"])
                S.op("dve", "tensor_tensor", out=blo[:], in0=bsf[:], in1=bhf[:], op=ALU.subtract,
                     reads=["bsf", "bsf0", "bsf32", "bhf"], writes=["blo"])
                S.op("dve", "tensor_copy", out=bs2[32:33, :], in_=blo[32:33, :], reads=["blo", "bs2"], writes=["bs2"])
                S.barrier()
            gbc = sb(es3, "gbc", 2048)
            fng = sb(es3, "fng", D)
            S.dma("sp", out=fng[:], in_=fng_d[:, :], writes=["fng"])
            S.dma("sp", out=gbc[:], in_=badag_d[:, :], writes=["gbc"])
            for n_, ch in enumerate([4, 5, 10, 11]):
                gi = 0 if ch < 6 else 1
                half = ch % 2
                ps = bk[n_ % 2]
                pk = ("bk", n_ % 2)
                for qd in range(4):
                    S.dma("sp", out=xq[:, qd * 1024:(qd + 1) * 1024], in_=wada_d[ch][:, qd * 1024:(qd + 1) * 1024],
                          writes=[("xq", qd)])
                    for k2 in range(2):
                        kc = qd * 2 + k2
                        S.op("pe", "matmul", out=ps[:, 0:512], lhsT=screp[:, kc * 128:(kc + 1) * 128],
                             rhs=xq[:, kc * 512:(kc + 1) * 512], start=(kc == 0), stop=(kc == 7),
                             reads=["screp", ("xq", qd)], writes=[pk])
                o = gi * 1024 + half * 512
                S.op("dve", "tensor_tensor", out=gbc[:, o:o + 512], in0=ps[:, 0:512],
                     in1=gbc[:, o:o + 512], op=ALU.add, reads=[pk, "gbc"], writes=["gbc"])
            for q in range(4):
                tq = q * 512
                with ExitStack() as esb:
                    hTq = sb(esb, "hTq", 8 * 512, BF16)
                    uT = sb(esb, "uT", 8 * 512, BF16)
                    sga = sb(esb, "sga", 8 * 512, BF16)
                    sgb = sb(esb, "sgb", 8 * 512, BF16)
                    yaT = sb(esb, "yaT", 8 * 512, BF16)
                    mT = sb(esb, "mT", 8 * 512, BF16)
                    w16 = sb(esb, "w16", 8 * 1024, BF16)
                    w16o = sb(esb, "w16o", 8 * 1024, BF16)
                    wst = [sb(esb, "wst%d" % i, 1024, BF16) for i in range(NWST)]
                    vg2 = [sb(esb, "vg%d" % i, 1024) for i in range(2)]
                    vg = vg2[0]
                    vnb = sb(esb, "vnb", 1024, BF16)
                    lng = sb(esb, "lng", 2048)
                    wsT = sb(esb, "wsT", 1024, BF16)
                    bst2 = sb(esb, "bst2", 12)
                    S.dma("sp", out=lng[:], in_=lng_d[:, :], writes=["lng"])
                    S.dma("sp", out=vg[:], in_=ws_d[:, :], writes=[("vg", 0)])
                    S.op("dve", "tensor_tensor", out=wsT[:].rearrange("p (g t) -> p g t", g=8),
                         in0=vg[:].rearrange("p (g t) -> p g t", g=8),
                         in1=trim.unsqueeze(1).broadcast_to([128, 8, 128]), op=ALU.mult,
                         reads=[("vg", 0), "consts"], writes=["wsT"])
                    for tb in range(4):
                        xk = ("xq", tb)
                        S.dma("sp", out=xq[:, tb * D:(tb + 1) * D], in_=x_d[q * 4 + tb], writes=[xk])
                        norm_block(xq[:, tb * D:(tb + 1) * D], xk, 0, hTq, "hTq", 512, tb * 128)
                    for ch in range(24):
                        if ch == 6:
                            load_bf(w16, "w16", WB_V, 8 * 1024)
                        if ch == 16:
                            load_bf(w16o, "w16o", WB_O, 8 * 1024)
                        k3 = ch % NWST
                        S.dma("sp", out=wst[k3][:], in_=wbf_d[WB_U + ch], reads=[("wbf", WB_U + ch)], writes=[("wst", k3)])
                        ps_ = bk[ch % 2]
                        pk = ("bk", ch % 2)
                        for kc in range(8):
                            S.op("pe", "matmul", out=ps_[:, 0:512], lhsT=wst[k3][:, kc * 128:(kc + 1) * 128],
                                 rhs=hTq[:, kc * 512:(kc + 1) * 512], start=(kc == 0), stop=(kc == 7),
                                 reads=[("wst", k3), ("hTq", kc)], writes=[pk])
                        dst, dk = [(uT, "uT"), (sga, "sga"), (sgb, "sgb")][ch // 8]
                        cc = ch % 8
                        S.op("act", "activation", out=dst[:, cc * 512:(cc + 1) * 512], in_=ps_[:, 0:512],
                             func=AF.Gelu_apprx_tanh if ch < 8 else AF.Sigmoid, reads=[pk], writes=[dk])

                    def v_mm(tb):
                        vb = [(0, 1), (2, 3)][tb % 2]
                        vgt, vgk = vg2[tb % 2], ("vg", tb % 2)
                        for n_ in range(2):
                            for kc in range(8):
                                S.op("pe", "matmul", out=bk[vb[n_]][:, 0:512],
                                     lhsT=hTq[:, kc * 512 + tb * 128:kc * 512 + (tb + 1) * 128],
                                     rhs=w16[:, kc * 1024 + n_ * 512:kc * 1024 + (n_ + 1) * 512],
                                     start=(kc == 0), stop=(kc == 7), reads=[("hTq", kc), ("w16", kc)], writes=[("bk", vb[n_])])
                            S.op("act", "activation", out=vgt[:, n_ * 512:(n_ + 1) * 512], in_=bk[vb[n_]][:, 0:512],
                                 func=AF.Gelu_apprx_tanh, reads=[("bk", vb[n_])], writes=[vgk])

                    def v_ln_mix(tb):
                        vgt, vgk = vg2[tb % 2], ("vg", tb % 2)
                        for n_ in range(2):
                            S.op("dve", "bn_stats", out=bst2[:, 6 * n_:6 * n_ + 6], in_=vgt[:, n_ * 512:(n_ + 1) * 512],
                                 reads=[vgk], writes=["bst2"])
                        S.op("dve", "bn_aggr", out=st[:, 8:10], in_=bst2[:], reads=["bst2"], writes=["vmv"])
                        S.op("dve", "tensor_scalar", out=st[:, 10:11], in0=st[:, 9:10], scalar1=EPS, scalar2=None,
                             op0=ALU.add, reads=["vmv"], writes=["vv_v"])
                        S.op("act", "activation", out=st[:, 10:11], in_=st[:, 10:11], func=AF.Sqrt,
                             reads=["vv_v"], writes=["vv_s"])
                        S.op("dve", "reciprocal", out=st[:, 10:11], in_=st[:, 10:11], reads=["vv_s"], writes=["vrstd"])
                        S.op("dve", "tensor_scalar", out=vgt[:], in0=vgt[:], scalar1=st[:, 8:9], scalar2=st[:, 10:11],
                             op0=ALU.subtract, op1=ALU.mult, reads=[vgk, "vmv", "vrstd"], writes=[vgk])
                        S.op("dve", "tensor_tensor", out=vgt[:], in0=vgt[:], in1=lng[:, 0:1024], op=ALU.mult,
                             reads=[vgk, "lng"], writes=[vgk])
                        S.op("dve", "tensor_tensor", out=vnb[:], in0=vgt[:], in1=lng[:, 1024:2048], op=ALU.add,
                             reads=[vgk, "lng"], writes=["vnb"])
                        for gh in range(2):
                            ps_ = bk[4 + gh]
                            pk = ("bk", 4 + gh)
                            for gg in range(4):
                                g = gh * 4 + gg
                                S.op("pe", "matmul", out=ps_[:, gg * 128:(gg + 1) * 128], lhsT=vnb[:, g * 128:(g + 1) * 128],
                                     rhs=wsT[:, g * 128:(g + 1) * 128], start=True, stop=False,
                                     reads=["vnb", "wsT"], writes=[pk])
                                S.op("pe", "matmul", out=ps_[:, gg * 128:(gg + 1) * 128], lhsT=onesz[:, 0:128],
                                     rhs=bs2[:, g * 128:(g + 1) * 128], start=False, stop=True,
                                     reads=["onesz", "bs2"], writes=[pk])
                            S.op("dve", "tensor_tensor",
                                 out=yaT[:].rearrange("p (c t) -> p c t", c=8)[:, gh * 4:gh * 4 + 4, tb * 128:(tb + 1) * 128],
                                 in0=ps_[:, 0:512].rearrange("p (c t) -> p c t", c=4),
                                 in1=uT[:].rearrange("p (c t) -> p c t", c=8)[:, gh * 4:gh * 4 + 4, tb * 128:(tb + 1) * 128],
                                 op=ALU.mult, reads=[pk, "uT"], writes=["yaT"])

                    v_mm(0)
                    for tb in range(4):
                        if tb + 1 < 4:
                            v_mm(tb + 1)
                        v_ln_mix(tb)

                    for cc in range(8):
                        for which, (wd, src, sk2) in enumerate([(wpa_d, yaT, "yaT"), (wpb_d, None, "ybT")]):
                            k3 = (2 * cc + which) % NWST
                            S.dma("sp", out=wst[k3][:], in_=wbf_d[WB_A + 2 * cc + which],
                                  reads=[("wbf", WB_A + 2 * cc + which)], writes=[("wst", k3)])
                            ps_ = bk[2 * (cc % 2) + which]
                            pk = ("bk", 2 * (cc % 2) + which)
                            for kc in range(8):
                                rhs = (yaT[:, kc * 512:(kc + 1) * 512] if which == 0
                                       else ybT[:, kc * 2048 + tq:kc * 2048 + tq + 512])
                                S.op("pe", "matmul", out=ps_[:, 0:512], lhsT=wst[k3][:, kc * 128:(kc + 1) * 128],
                                     rhs=rhs, start=(kc == 0), stop=(kc == 7), reads=[("wst", k3), sk2], writes=[pk])
                        S.op("dve", "tensor_tensor", out=tmpa[:], in0=bk[2 * (cc % 2)][:, 0:512], in1=sga[:, cc * 512:(cc + 1) * 512],
                             op=ALU.mult, reads=[("bk", 2 * (cc % 2)), "sga"], writes=["tmpa"])
                        S.op("dve", "tensor_tensor", out=tmpb[:], in0=bk[2 * (cc % 2) + 1][:, 0:512], in1=sgb[:, cc * 512:(cc + 1) * 512],
                             op=ALU.mult, reads=[("bk", 2 * (cc % 2) + 1), "sgb"], writes=["tmpb"])
                        S.op("dve", "tensor_tensor", out=mT[:, cc * 512:(cc + 1) * 512], in0=tmpa[:], in1=tmpb[:],
                             op=ALU.add, reads=["tmpa", "tmpb"], writes=["mT"])
                    for tb in range(4):
                        xk = ("xq", tb)
                        for n_ in range(2):
                            for kc in range(8):
                                S.op("pe", "matmul", out=bk[[4, 0][tb % 2] + n_][:, 0:512],
                                     lhsT=mT[:, kc * 512 + tb * 128:kc * 512 + (tb + 1) * 128],
                                     rhs=w16o[:, kc * 1024 + n_ * 512:kc * 1024 + (n_ + 1) * 512],
                                     start=(kc == 0), stop=(kc == 7), reads=["mT", ("w16o", kc)],
                                     writes=[("bk", [4, 0][tb % 2] + n_)])
                            t_, tk = (tmpa, "tmpa") if n_ == 0 else (tmpb, "tmpb")
                            S.op("dve", "tensor_tensor", out=t_[:], in0=bk[[4, 0][tb % 2] + n_][:, 0:512],
                                 in1=gbc[:, n_ * 512:(n_ + 1) * 512], op=ALU.mult,
                                 reads=[("bk", [4, 0][tb % 2] + n_), "gbc"], writes=[tk])
                            xo = tb * D + n_ * 512
                            S.op("dve", "tensor_tensor", out=xq[:, xo:xo + 512], in0=xq[:, xo:xo + 512], in1=t_[:],
                                 op=ALU.add, reads=[tk, xk], writes=[xk])
                    S.barrier()
                with ExitStack() as esf:
                    h2T = sb(esf, "h2T", 8 * 512, BF16)
                    aT = sb(esf, "aT", NFC * 512, BF16)
                    wffo = sb(esf, "wffo", NFC * 1024, BF16)
                    wfi = [sb(esf, "wfi%d" % i, 2048, BF16) for i in range(NWFI)]
                    sgt = [sb(esf, "sgt%d" % i, 512) for i in range(2)]
                    obuf = [sb(esf, "obuf%d" % i, D) for i in range(1)]
                    for tb in range(4):
                        norm_block(xq[:, tb * D:(tb + 1) * D], ("xq", tb), 16, h2T, "h2T", 512, tb * 128)
                    for cp in range(NFC):
                        load_bf(wfi[cp % NWFI], ("wfi", cp % NWFI), WB_FI + 2 * cp, 2048)
                        load_bf(wffo, "wffo", WB_FO, NFC * 1024, ks=[cp])
                        for half in range(2):
                            ps_ = bk[2 * (cp % 2) + half]
                            pk = ("bk", 2 * (cp % 2) + half)
                            for kc in range(8):
                                S.op("pe", "matmul", out=ps_[:, 0:512],
                                     lhsT=wfi[cp % NWFI][:, kc * 256 + half * 128:kc * 256 + (half + 1) * 128],
                                     rhs=h2T[:, kc * 512:(kc + 1) * 512], start=(kc == 0), stop=(kc == 7),
                                     reads=[(("wfi", cp % NWFI), kc // 4), ("h2T", kc)], writes=[pk])
                        S.op("act", "activation", out=sgt[cp % 2][:], in_=bk[2 * (cp % 2)][:, 0:512], func=AF.Silu,
                             reads=[("bk", 2 * (cp % 2))], writes=[("sgt", cp % 2)])
                        S.op("dve", "tensor_tensor", out=aT[:, cp * 512:(cp + 1) * 512], in0=bk[2 * (cp % 2) + 1][:, 0:512],
                             in1=sgt[cp % 2][:], op=ALU.mult, reads=[("bk", 2 * (cp % 2) + 1), ("sgt", cp % 2)],
                             writes=["aT"])
                    for tb in range(4):
                        xk = ("xq", tb)
                        for n_ in range(2):
                            for kc in range(NFC):
                                S.op("pe", "matmul", out=bk[[4, 0][tb % 2] + n_][:, 0:512],
                                     lhsT=aT[:, kc * 512 + tb * 128:kc * 512 + (tb + 1) * 128],
                                     rhs=wffo[:, kc * 1024 + n_ * 512:kc * 1024 + (n_ + 1) * 512],
                                     start=(kc == 0), stop=(kc == NFC - 1), reads=["aT", ("wffo", kc)],
                                     writes=[("bk", [4, 0][tb % 2] + n_)])
                            t_, tk = (tmpa, "tmpa") if n_ == 0 else (tmpb, "tmpb")
                            S.op("dve", "tensor_tensor", out=t_[:], in0=bk[[4, 0][tb % 2] + n_][:, 0:512],
                                 in1=gbc[:, 1024 + n_ * 512:1024 + (n_ + 1) * 512], op=ALU.mult,
                                 reads=[("bk", [4, 0][tb % 2] + n_), "gbc"], writes=[tk])
                            xo = tb * D + n_ * 512
                            S.op("dve", "tensor_tensor", out=xq[:, xo:xo + 512], in0=xq[:, xo:xo + 512], in1=t_[:],
                                 op=ALU.add, reads=[tk, xk], writes=[xk])
                        ob_, ok_ = obuf[0], ("obuf", 0)
                        S.op("act", "activation", out=junkb[:], in_=xq[:, tb * D:(tb + 1) * D], func=AF.Square,
                             accum_out=st[:, 12:13], reads=[xk], writes=["junkb", "fss"])
                        rstd_from_sum(12, 13, 1.0 / D, "fss", "frstd")
                        S.op("act", "activation", out=ob_[:], in_=xq[:, tb * D:(tb + 1) * D], func=AF.Copy,
                             scale=st[:, 13:14], reads=[xk, "frstd"], writes=[ok_])
                        S.op("dve", "tensor_tensor", out=ob_[:], in0=ob_[:], in1=fng[:], op=ALU.mult,
                             reads=[ok_, "fng"], writes=[ok_])
                        S.dma("pool", out=out_d[q * 4 + tb], in_=ob_[:], reads=[ok_], writes=[("out", q * 4 + tb)])
                    S.barrier()
        S.wait_all("sp")
    return nc


_NC = None


def _own_blocks(p):
    a = [0, 3] if p == 0 else [1, 2]
    return [4 * g + o for g in range(8) for o in a]


def _kc_layout(w, K):
    n = w.shape[1]
    return np.ascontiguousarray(w.reshape(K // 128, 128, n).transpose(1, 0, 2).reshape(128, (K // 128) * n))


def _chunked(w, K, cw):
    n = w.shape[1]
    a = w.reshape(K // 128, 128, n // cw, cw).transpose(2, 1, 0, 3)
    return np.ascontiguousarray(a.reshape(n // cw, 128, (K // 128) * cw))


def _prep(x, c, positions, w_ada, b_ada, norm1_g, w_in, gmlp_ln_g, gmlp_ln_b, gmlp_w_s, gmlp_b_s,
          idx_k_ln_g, idx_k_ln_b, w_proj_a, w_proj_b, w_out, norm2_g, w_ffn_in, w_ffn_out, final_norm_g):
    f = lambda a: np.asarray(a, dtype=np.float32)
    x = f(x); c = f(c); positions = np.asarray(positions, dtype=np.int32)
    w_ada = f(w_ada)[0]; b_ada = f(b_ada)[0]; w_in = f(w_in)[0]
    o = np.cumsum([0, 1024, 1024, 1024, 256, 256, 512, 64, 8, 1024, 1024])
    au, av, q_, k_, v_, qi_, ki_, wi_, ga_, gb_ = [w_in[:, o[i]:o[i + 1]] for i in range(10)]
    rep = lambda v: np.ascontiguousarray(np.broadcast_to(f(v).reshape(1, -1), (128, f(v).size)))
    consts = np.zeros((128, NCONST), np.float32)
    ar = np.arange(128, dtype=np.float32)
    consts[:, OFF_IOTA:OFF_IOTA + 128] = ar[None, :]
    consts[:, OFF_IDENT:OFF_IDENT + 128] = np.eye(128, dtype=np.float32)
    consts[:, OFF_TRIB:OFF_TRIB + 128] = np.where(ar[None, :] > ar[:, None], -BIG, 0.0)
    consts[:, OFF_TRIM:OFF_TRIM + 128] = (ar[None, :] >= ar[:, None]).astype(np.float32)
    consts[:, OFF_INVQ:OFF_INVQ + 16] = (np.float32(500000.0) ** (-np.arange(0, 32, 2, dtype=np.float32) / np.float32(32)))[None, :]
    consts[:, OFF_INVI:OFF_INVI + 8] = (np.float32(500000.0) ** (-np.arange(0, 16, 2, dtype=np.float32) / np.float32(16)))[None, :]
    p2 = [2.0 ** (-k) for k in range(NIT)] + [2.0 ** (-(NIT - 1))]
    consts[:, OFF_POW2:OFF_POW2 + NIT + 1] = np.array(p2, np.float32)[None, :]

    shared = {
        "wada": _chunked(w_ada, 1024, 512),
        "badaf": np.ascontiguousarray(b_ada.reshape(48, 128).T),
        "badag": np.concatenate([rep(b_ada[2048:3072]), rep(b_ada[5120:6144])], axis=1),
        "gfm": np.concatenate([f(norm1_g)[0].reshape(8, 128).T, f(norm2_g)[0].reshape(8, 128).T], axis=1).copy(),
        "fng": rep(final_norm_g),
        "lng": np.concatenate([rep(f(gmlp_ln_g)[0]), rep(f(gmlp_ln_b)[0])], axis=1),
        "kln": np.concatenate([rep(f(idx_k_ln_g)[0]), rep(f(idx_k_ln_g)[0]), rep(f(idx_k_ln_b)[0]), rep(f(idx_k_ln_b)[0])], axis=1),
        "consts": consts,
        "wkv": _kc_layout(np.concatenate([k_, v_, ki_, ki_], axis=1), 1024),
        "wq": _kc_layout(np.concatenate([q_, qi_, wi_], axis=1), 1024),
        "wv": _kc_layout(av, 1024),
        "wuvg": _chunked(np.concatenate([au, ga_, gb_], axis=1), 1024, 128),
        "wpa": _chunked(f(w_proj_a)[0], 1024, 128),
        "wpb": _chunked(f(w_proj_b)[0], 1024, 128),
        "wout": _kc_layout(f(w_out)[0], 1024),
        "ws": np.ascontiguousarray(f(gmlp_w_s)[0].transpose(2, 0, 1).reshape(128, 1024)),
        "bs": np.ascontiguousarray(f(gmlp_b_s)[0].reshape(1, 1024)),
        "wffo": _kc_layout(f(w_ffn_out)[0], DFF),
    }
    wfi = f(w_ffn_in)[0]
    gcols = wfi[:, :DFF].reshape(1024, NFC, 128)
    ucols = wfi[:, DFF:].reshape(1024, NFC, 128)
    shared["wffi"] = _chunked(np.concatenate([gcols, ucols], axis=2).reshape(1024, NFC * 256), 1024, 256)

    in_maps = []
    owns = []
    for core in range(8):
        b, p = core // 2, core % 2
        own = _own_blocks(p)
        oth = _own_blocks(1 - p)
        perm = own + oth
        owns.append(own)
        xb = x[b].reshape(NB, 128, D)[perm]
        posb = positions[b].reshape(NB, 128)[perm]
        qrel = np.stack([(own[j] - oth[j]) * 128 + np.arange(128) for j in range(NOWN)], axis=1).astype(np.float32)
        m = dict(shared)
        m["x"] = np.ascontiguousarray(xb)
        m["pos"] = np.ascontiguousarray(posb.T.astype(np.int32))
        m["cT"] = np.ascontiguousarray(c[b].reshape(8, 128).T)
        m["qrel"] = np.ascontiguousarray(qrel)
        in_maps.append(m)
    return in_maps, owns


def kernel(**inputs):
    global _NC
    in_maps, owns = _prep(**inputs)
    if _NC is None:
        _NC = build()
    res = run_bass_kernel_spmd(_NC, in_maps, core_ids=list(range(8)))
    out = np.zeros((4, NB, 128, D), np.float32)
    for core in range(8):
        b = core // 2
        o_ = np.asarray(res.results[core]["out"]).reshape(NOWN, 128, D)
        for j, blk in enumerate(owns[core]):
            out[b, blk] = o_[j]
    return out.reshape(4, NB * 128, D)
```
